# Optimizing a Trainium2 kernel written in Bass

```python
import math
import jax, jax.numpy as jnp
from jax import lax
import numpy as np

D_MODEL = 2048
BATCH = 2
SEQ = 4096
DEPTH = 4

N_HEADS = 8
HEAD_DIM = 128
ATTN_W = N_HEADS * HEAD_DIM
MOBA_BLOCK = 256
MOBA_TOPK = 3
MOBA_QCHUNK = 64
N_BUCKETS = 32
MAX_DISTANCE = 128
LRU_W = D_MODEL // 2
LRU_BLOCKS = 8
LRU_BW = LRU_W // LRU_BLOCKS
LRU_CONV = 4
LRU_C = 8.0
SC_W = D_MODEL // 2
SC_CONV = 3
D_FF = 4 * D_MODEL
N_BRANCH = 3
EPS = 1e-6
IN_SPLITS = (ATTN_W, ATTN_W, ATTN_W, LRU_W, LRU_W, SC_W, SC_W, SC_W, N_BRANCH * D_MODEL)
N_IN = sum(IN_SPLITS)

kernel_name = "hybrid_moba_rglru_shortconv_trunk"


def rms_norm(x, g):
    xf = x.astype(jnp.float32)
    y = xf * lax.rsqrt(jnp.mean(xf * xf, axis=-1, keepdims=True) + EPS)
    return (y * g.astype(jnp.float32)).astype(x.dtype)


def causal_depthwise_conv(x, w):
    k = w.shape[0]
    return lax.conv_general_dilated(
        x, w[:, None, :].astype(x.dtype), window_strides=(1,), padding=[(k - 1, 0)],
        dimension_numbers=('NWC', 'WIO', 'NWC'), feature_group_count=x.shape[-1])


def t5_bucket(dist):
    n = jnp.maximum(dist, 0)
    max_exact = N_BUCKETS // 2
    nf = jnp.maximum(n, 1).astype(jnp.float32)
    large = max_exact + (jnp.log(nf / max_exact) / math.log(MAX_DISTANCE / max_exact)
                         * (N_BUCKETS - max_exact)).astype(jnp.int32)
    large = jnp.minimum(large, N_BUCKETS - 1)
    return jnp.where(n < max_exact, n, large)


def moba_attention(q, k, v, rel_table):
    bsz, seq, n_heads, d_head = q.shape
    n_blk = -(-seq // MOBA_BLOCK)
    s_pad = n_blk * MOBA_BLOCK
    topk = min(MOBA_TOPK, n_blk)
    pad = ((0, 0), (0, s_pad - seq), (0, 0), (0, 0))
    qh = jnp.pad(q * (d_head ** -0.5), pad).transpose(0, 2, 1, 3)
    kh = jnp.pad(k, pad).transpose(0, 2, 1, 3)
    vh = jnp.pad(v, pad).transpose(0, 2, 1, 3)
    kb = kh.reshape(bsz, n_heads, n_blk, MOBA_BLOCK, d_head)
    vb = vh.reshape(bsz, n_heads, n_blk, MOBA_BLOCK, d_head)
    k_mean = jnp.mean(kb.astype(jnp.float32), axis=3)
    gate = jnp.einsum('bhsd,bhnd->bhsn', qh.astype(jnp.float32), k_mean)
    q_blk = jnp.arange(s_pad) // MOBA_BLOCK
    past = jnp.arange(n_blk)[None, :] < q_blk[:, None]
    gate = jnp.where(past, gate, -jnp.inf)
    _, sel = lax.top_k(gate, topk)
    slot_ok = jnp.arange(topk)[None, :] < q_blk[:, None]
    b_idx = jnp.arange(bsz)[:, None, None, None]
    h_idx = jnp.arange(n_heads)[None, :, None, None]
    tbl_hb = rel_table.T
    blk_pos = jnp.arange(MOBA_BLOCK)

    def chunk(c):
        start = c * MOBA_QCHUNK
        qc = lax.dynamic_slice_in_dim(qh, start, MOBA_QCHUNK, axis=2)
        sel_c = lax.dynamic_slice_in_dim(sel, start, MOBA_QCHUNK, axis=2)
        ok_c = lax.dynamic_slice_in_dim(slot_ok, start, MOBA_QCHUNK, axis=0)
        q_pos = start + jnp.arange(MOBA_QCHUNK)
        kg = kb[b_idx, h_idx, sel_c]
        vg = vb[b_idx, h_idx, sel_c]
        s_past = jnp.einsum('bhqd,bhqnkd->bhqnk', qc, kg, preferred_element_type=jnp.float32)
        k_pos = sel_c[..., None] * MOBA_BLOCK + blk_pos
        bkt = t5_bucket(q_pos[:, None, None] - k_pos)
        s_past = s_past + tbl_hb[h_idx[..., None], bkt]
        s_past = jnp.where(ok_c[:, :, None], s_past, -jnp.inf)
        own = start // MOBA_BLOCK
        k_own = lax.dynamic_index_in_dim(kb, own, axis=2, keepdims=False)
        v_own = lax.dynamic_index_in_dim(vb, own, axis=2, keepdims=False)
        s_own = jnp.einsum('bhqd,bhkd->bhqk', qc, k_own, preferred_element_type=jnp.float32)
        dist = q_pos[:, None] - (own * MOBA_BLOCK + blk_pos)[None, :]
        s_own = s_own + rel_table[t5_bucket(dist)].transpose(2, 0, 1)
        s_own = jnp.where(dist >= 0, s_own, -jnp.inf)
        logits = jnp.concatenate(
            [s_past.reshape(bsz, n_heads, MOBA_QCHUNK, topk * MOBA_BLOCK), s_own], axis=-1)
        p = jax.nn.softmax(logits, axis=-1)
        p_past = p[..., :topk * MOBA_BLOCK].reshape(bsz, n_heads, MOBA_QCHUNK, topk, MOBA_BLOCK)
        p_own = p[..., topk * MOBA_BLOCK:]
        o = (jnp.einsum('bhqnk,bhqnkd->bhqd', p_past.astype(vg.dtype), vg,
                        preferred_element_type=jnp.float32)
             + jnp.einsum('bhqk,bhkd->bhqd', p_own.astype(v_own.dtype), v_own,
                          preferred_element_type=jnp.float32))
        return o.astype(q.dtype)

    out = lax.map(chunk, jnp.arange(s_pad // MOBA_QCHUNK))
    out = out.transpose(1, 0, 3, 2, 4).reshape(bsz, s_pad, n_heads * d_head)
    return out[:, :seq]


def _lru_combine(c1, c2):
    a1, b1 = c1
    a2, b2 = c2
    return a1 * a2, a2 * b1 + b2


def rg_lru_branch(xr, gate_br, conv_w, conv_b, wa, ba, wx, bx, lam):
    xc = causal_depthwise_conv(xr, conv_w) + conv_b
    bsz, seq, width = xc.shape
    xg = xc.reshape(bsz, seq, LRU_BLOCKS, LRU_BW)
    r = jax.nn.sigmoid((jnp.einsum('bsgi,gij->bsgj', xg, wa).reshape(bsz, seq, width) + ba)
                       .astype(jnp.float32))
    i = jax.nn.sigmoid((jnp.einsum('bsgi,gij->bsgj', xg, wx).reshape(bsz, seq, width) + bx)
                       .astype(jnp.float32))
    log_a = -LRU_C * r * jax.nn.softplus(-lam.astype(jnp.float32))
    a = jnp.exp(log_a)
    mult = jnp.sqrt(-jnp.expm1(2.0 * log_a))
    u = mult * i * xc.astype(jnp.float32)
    _, h = lax.associative_scan(_lru_combine, (a, u), axis=1)
    return (h * jax.nn.gelu(gate_br.astype(jnp.float32))).astype(xr.dtype)


def short_conv_branch(b, c, xs, conv_w):
    return b * causal_depthwise_conv(c * xs, conv_w)


def setup_inputs(seed: int = 0) -> dict:
    key = jax.random.key(seed)
    ks = jax.random.split(key, 24)
    f32 = jnp.float32
    nrm = lambda k, shape, fan_in: jax.random.normal(k, shape, f32) * (fan_in ** -0.5)
    u = jax.random.uniform(ks[12], (DEPTH, LRU_W), f32, minval=0.9, maxval=0.999)
    a_base = u ** (1.0 / LRU_C)
    lam = jnp.log(a_base) - jnp.log1p(-a_base)
    return {
        "x": jax.random.normal(ks[0], (BATCH, SEQ, D_MODEL), f32),
        "norm_mix_g": 1.0 + 0.02 * jax.random.normal(ks[1], (DEPTH, D_MODEL), f32),
        "w_in": nrm(ks[2], (DEPTH, D_MODEL, N_IN), D_MODEL),
        "gate_b": 0.02 * jax.random.normal(ks[3], (DEPTH, N_BRANCH * D_MODEL), f32),
        "rel_table": 0.1 * jax.random.normal(ks[4], (N_BUCKETS, N_HEADS), f32),
        "w_attn_o": nrm(ks[5], (DEPTH, ATTN_W, D_MODEL), ATTN_W),
        "lru_conv_w": nrm(ks[6], (DEPTH, LRU_CONV, LRU_W), LRU_CONV),
        "lru_conv_b": 0.02 * jax.random.normal(ks[7], (DEPTH, LRU_W), f32),
        "lru_wa": nrm(ks[8], (DEPTH, LRU_BLOCKS, LRU_BW, LRU_BW), LRU_BW),
        "lru_ba": 0.02 * jax.random.normal(ks[9], (DEPTH, LRU_W), f32),
        "lru_wx": nrm(ks[10], (DEPTH, LRU_BLOCKS, LRU_BW, LRU_BW), LRU_BW),
        "lru_bx": 0.02 * jax.random.normal(ks[11], (DEPTH, LRU_W), f32),
        "lru_lambda": lam,
        "w_lru_o": nrm(ks[13], (DEPTH, LRU_W, D_MODEL), LRU_W),
        "sc_conv_w": nrm(ks[14], (DEPTH, SC_CONV, SC_W), SC_CONV),
        "w_sc_o": nrm(ks[15], (DEPTH, SC_W, D_MODEL), SC_W),
        "w_out": nrm(ks[16], (DEPTH, D_MODEL, D_MODEL), D_MODEL),
        "norm_mlp_g": 1.0 + 0.02 * jax.random.normal(ks[17], (DEPTH, D_MODEL), f32),
        "w_mlp_up": nrm(ks[18], (DEPTH, D_MODEL, D_FF), D_MODEL),
        "w_mlp_down": nrm(ks[19], (DEPTH, D_FF, D_MODEL), D_FF),
        "final_g": 1.0 + 0.02 * jax.random.normal(ks[20], (D_MODEL,), f32),
    }


def reference(x, norm_mix_g, w_in, gate_b, rel_table, w_attn_o, lru_conv_w, lru_conv_b,
              lru_wa, lru_ba, lru_wx, lru_bx, lru_lambda, w_lru_o, sc_conv_w, w_sc_o,
              w_out, norm_mlp_g, w_mlp_up, w_mlp_down, final_g):
    bsz, seq, _ = x.shape
    offs = np.cumsum(IN_SPLITS)[:-1].tolist()
    for l in range(DEPTH):
        h = rms_norm(x, norm_mix_g[l])
        proj = h @ w_in[l]
        q, k, v, xr, gr, sc_b, sc_c, sc_x, g_pre = jnp.split(proj, offs, axis=-1)
        heads = (bsz, seq, N_HEADS, HEAD_DIM)
        y_a = moba_attention(q.reshape(heads), k.reshape(heads), v.reshape(heads),
                             rel_table) @ w_attn_o[l]
        y_b = rg_lru_branch(xr, gr, lru_conv_w[l], lru_conv_b[l], lru_wa[l], lru_ba[l],
                            lru_wx[l], lru_bx[l], lru_lambda[l]) @ w_lru_o[l]
        y_c = short_conv_branch(sc_b, sc_c, sc_x, sc_conv_w[l]) @ w_sc_o[l]
        g_a, g_b, g_c = jnp.split(jax.nn.sigmoid(g_pre + gate_b[l]), N_BRANCH, axis=-1)
        x = x + (g_a * y_a + g_b * y_b + g_c * y_c) @ w_out[l]
        h = rms_norm(x, norm_mlp_g[l])
        x = x + jnp.square(jax.nn.relu(h @ w_mlp_up[l])) @ w_mlp_down[l]
    return rms_norm(x, final_g)
```

```python
import math
from contextlib import ExitStack
import numpy as np
import ml_dtypes
import concourse.bass as bass
import concourse.mybir as mybir
from concourse.bass_utils import run_bass_kernel_spmd

F32, BF16 = mybir.dt.float32, mybir.dt.bfloat16
AF = mybir.ActivationFunctionType
ALU = mybir.AluOpType
AX = mybir.AxisListType

D = 2048
KC = 16
T = 1024
NIN = 14336
DEPTH = 4
NH = 8
DFF = 8192
EPS = 1e-6
NEG = -1.0e30
RB = 2056
SCALE = 128 ** -0.5

OFF_Q, OFF_K, OFF_V, OFF_XR, OFF_GR, OFF_SCB, OFF_SCC, OFF_SCX, OFF_G = (
    0, 1024, 2048, 3072, 4096, 5120, 6144, 7168, 8192)

LC = 168
C_G1, C_G2, C_GB, C_CW, C_CB, C_BA, C_BX, C_LAM, C_SW = 0, 16, 32, 80, 112, 120, 128, 136, 144
C_FG = DEPTH * LC
C_CFAR = C_FG + 16
C_OHP = C_CFAR + 8
C_MLT = C_OHP + 3
C_NML = C_MLT + 4
C_HKILL = C_NML + 4
C_HNONE = C_HKILL + 3
NCOL = C_HNONE + 1 + 3


class Res:
    __slots__ = ("w", "rd", "const")

    def __init__(self, const=False):
        self.w = None
        self.rd = []
        self.const = const


class Eng:
    def __init__(self, obj, sem, name):
        self.obj = obj
        self.sem = sem
        self.cnt = 0
        self.seen = {}
        self.name = name
        self.dsems = []
        self.dcnt = []
        self.dnext = 0


class K:
    def __init__(self, nc, es, depth):
        self.nc = nc
        self.es = es
        self.depth = depth
        mk = lambda n: es.enter_context(nc.semaphore(n))
        self.pe = Eng(nc.tensor, mk("s_pe"), "pe")
        self.act = Eng(nc.scalar, mk("s_act"), "act")
        self.dve = Eng(nc.vector, mk("s_dve"), "dve")
        self.pool = Eng(nc.gpsimd, mk("s_pool"), "pool")
        self.sp = Eng(nc.sync, mk("s_sp"), "sp")
        for q, n in ((self.sp, 24), (self.pool, 4)):
            for i in range(n):
                q.dsems.append(mk(f"d_{q.name}{i}"))
                q.dcnt.append(0)
        self.ccsem = mk("s_cc")
        self.cccnt = 0
        self.flip = 0

    def _need(self, rd, wr):
        toks = []
        for r in rd:
            if r.w is not None:
                toks.append(r.w)
        for w in wr:
            if w.w is not None:
                toks.append(w.w)
            toks.extend(w.rd)
        return toks

    def _wait(self, e, toks):
        best = {}
        for (sem, val) in toks:
            if e.seen.get(sem, 0) < val and best.get(sem, 0) < val:
                best[sem] = val
        for sem, val in best.items():
            if sem is e.sem and e is self.pe:
                continue
            e.obj.wait_ge(sem, val)
            e.seen[sem] = val

    def _commit(self, tok, rd, wr):
        for r in rd:
            if not r.const:
                r.rd.append(tok)
        for w in wr:
            w.w = tok
            w.rd = []

    def op(self, e, fn, rd=(), wr=()):
        self._wait(e, self._need(rd, wr))
        ins = fn(e.obj)
        e.cnt += 1
        ins.then_inc(e.sem, 1)
        tok = (e.sem, e.cnt)
        self._commit(tok, rd, wr)
        return tok

    def mm(self, mms, rd=(), wr=()):
        e = self.pe
        self._wait(e, self._need(rd, wr))
        ins = None
        for f in mms:
            ins = f(e.obj)
        e.cnt += 1
        ins.then_inc(e.sem, 1)
        tok = (e.sem, e.cnt)
        self._commit(tok, rd, wr)
        return tok

    def dma(self, q, out, in_, rd=(), wr=(), **kw):
        i = q.dnext
        q.dnext = (q.dnext + 1) % len(q.dsems)
        sem = q.dsems[i]
        toks = self._need(rd, wr)
        if q.dcnt[i] > 0:
            toks.append((sem, 16 * q.dcnt[i]))
        self._wait(q, toks)
        q.dcnt[i] += 1
        q.obj.dma_start(out=out, in_=in_, **kw).then_inc(sem, 16)
        tok = (sem, 16 * q.dcnt[i])
        self._commit(tok, rd, wr)
        return tok

    def allgather(self, cin, cout, rd=(), wr=()):
        q = self.pool
        self._wait(q, self._need(rd, wr))
        self.cccnt += 1
        q.obj.collective_compute(
            "AllGather", ALU.bypass, replica_groups=[[0, 1, 2, 3], [4, 5, 6, 7]],
            ins=[cin.ap().opt()], outs=[cout.ap().opt()]).then_inc(self.ccsem)
        tok = (self.ccsem, self.cccnt)
        self._commit(tok, rd, wr)
        return tok

    def ev(self):
        self.flip ^= 1
        return self.act if self.flip else self.dve


import os as _os
GATHER = bool(_os.environ.get('K_GATHER'))


def build(depth=DEPTH):
    nc = bass.Bass("TRN2", target_bir_lowering=False)
    es = ExitStack()
    k = K(nc, es, depth)
    pe, act, dve, pool, sp = k.pe, k.act, k.dve, k.pool, k.sp

    def din(name, shape, dt=F32):
        return nc.dram_tensor(name, list(shape), dt, kind="ExternalInput").ap()

    xT_in = din("xT", [D, T])
    WSHAPES = {"w_in": (D, NIN), "w_attn_o": (1024, D), "w_lru_o": (1024, D), "w_sc_o": (1024, D),
               "w_out": (D, D), "w_mlp_up": (D, DFF), "w_mlp_down": (DFF, D)}
    wsrc = {}
    r_wg = {}
    wshard = {}
    for name, (R_, C_) in WSHAPES.items():
        if GATHER:
            wshard[name] = din(name, [depth, R_ // 8, C_])
            wsrc[name] = []
            for l in range(depth):
                g_ = nc.dram_tensor(f"{name}_g{l}", [R_, C_], F32)
                wsrc[name].append(g_)
                r_wg[(name, l)] = Res()
        else:
            full = din(name, [depth, R_, C_])
            wsrc[name] = [full[l] for l in range(depth)]
            r_wg.update({(name, l): Res() for l in range(depth)})
    lru_wa = din("lru_wa", [depth, 8, 128, 128])
    lru_wx = din("lru_wx", [depth, 8, 128, 128])

    def wap(name, l):
        t_ = wsrc[name][l]
        return t_.ap() if GATHER else t_

    def gather_weights(l):
        if not GATHER:
            return
        for name, (R_, C_) in WSHAPES.items():
            bnc = nc.dram_tensor(f"{name}_b{l}", [R_ // 8, C_], F32)
            rb = Res()
            k.dma(sp, bnc.ap()[:, :], wshard[name][l], wr=[rb])
            q = k.pool
            k._wait(q, k._need([rb], [r_wg[(name, l)]]))
            k.cccnt += 1
            q.obj.collective_compute("AllGather", ALU.bypass, replica_groups=[list(range(8))],
                                     ins=[bnc.ap().opt()], outs=[wsrc[name][l].ap().opt()]).then_inc(k.ccsem)
            tok = (k.ccsem, k.cccnt)
            k._commit(tok, [rb], [r_wg[(name, l)]])
    cols_in = din("cols", [128, NCOL])
    mrow_in = din("mrow", [128, 128])
    bdiag_in = din("bdiag", [128, NH * 128], BF16)
    bprev_in = din("bprev", [128, NH * 128], BF16)
    ident_in = din("ident", [128, 128], BF16)
    outT = nc.dram_tensor("outT", [D, T], F32, kind="ExternalOutput").ap()

    DBG = _os.environ.get("K_DBG", "")
    dk = dict(kind="ExternalInput") if DBG == "attn" else {}
    ck = [[nc.dram_tensor(f"ck{i}_{g}", [512, 1024], BF16, **(dk if i == 0 else {})) for g in range(2)] for i in range(2)]
    ckg = [[nc.dram_tensor(f"ckg{i}_{g}", [4 * 512, 1024], BF16, **(dk if i == 0 else {})) for g in range(2)] for i in range(2)]
    cv = [[nc.dram_tensor(f"cv{i}_{g}", [1024, 512], BF16, **(dk if i == 0 else {})) for g in range(2)] for i in range(2)]
    cvg = [[nc.dram_tensor(f"cvg{i}_{g}", [4 * 1024, 512], BF16, **(dk if i == 0 else {})) for g in range(2)] for i in range(2)]
    if DBG == "attn":
        qT_dbg = nc.dram_tensor("qT_dbg", [1024, 1024], BF16, kind="ExternalInput").ap()
        aT_dbg = nc.dram_tensor("aT_dbg", [1024, 1024], BF16, kind="ExternalOutput").ap()
    ctl = [nc.dram_tensor(f"ctl{i}", [8, 1024], BF16) for i in range(2)]
    ctg = [nc.dram_tensor(f"ctg{i}", [32, 1024], BF16) for i in range(2)]
    r_ck = [[Res(), Res()] for _ in range(2)]
    r_ckg = [[Res(), Res()] for _ in range(2)]
    r_cv = [[Res(), Res()] for _ in range(2)]
    r_cvg = [[Res(), Res()] for _ in range(2)]
    r_ctl = [Res(), Res()]
    r_ctg = [Res(), Res()]
    cin2 = [nc.dram_tensor(f"cinb{i}", [128, 16], F32) for i in range(2)]
    cout2 = [nc.dram_tensor(f"coutb{i}", [512, 16], F32) for i in range(2)]
    spill = nc.dram_tensor("spill", [5, 1024, 1024], BF16).ap()
    hsp = nc.dram_tensor("hsp", [D, T], BF16).ap()
    r_cin2 = [Res(), Res()]
    r_cout2 = [Res(), Res()]
    r_spill = [[Res() for _ in range(8)] for _ in range(5)]
    r_hsp = Res()
    SP_XR, SP_GR, SP_SCB, SP_SCC, SP_SCX = 0, 1, 2, 3, 4

    sb = lambda name, shape, dt: es.enter_context(nc.sbuf_tensor(name, list(shape), dt))
    XT = sb("XT", [128, KC, T], F32)
    r_xt = [[Res(), Res()] for _ in range(KC)]
    RX = sb("RX", [128, 16384], BF16)
    ABC = sb("ABC", [128, 3, 8, T], BF16)
    WS = sb("WS", [128, 2, 8192], BF16)
    WK = sb("WK", [128, 10240], BF16)
    COLS = sb("COLS", [128, NCOL], F32)
    MROW = sb("MROW", [128, 128], F32)
    IDN = sb("IDN", [128, 128], BF16)
    ONES = sb("ONES", [128, 128], F32)
    ZC = sb("ZC", [128, 2], F32)
    LW = sb("LW", [128, 2, 8, 128], BF16)
    SM = sb("SM", [128, 480], F32)
    TLS = sb("TLS", [128, 8, 8], BF16)
    r_const = Res(const=True)
    r_lw = Res()
    r_tls = Res()
    PS = es.enter_context(nc.psum_tensor("PS", [128, 8, 512], F32))
    r_ps = [Res() for _ in range(8)]

    HT = RX[:, :].rearrange("p (c t) -> p c t", c=KC)
    r_ht = [Res() for _ in range(KC)]

    def rx_res(lo, hi):
        return [r_ht[i] for i in range(lo // 1024, (hi + 1023) // 1024)]

    r_abc = [[Res() for _ in range(8)] for _ in range(3)]
    r_wk = [Res() for _ in range(10)]
    region = {"WK": list(r_wk), "RX": list(r_ht)}

    def inherit(new, old):
        best = {}
        for o in old:
            for tk in ([o.w] if o.w is not None else []) + o.rd:
                if best.get(tk[0], 0) < tk[1]:
                    best[tk[0]] = tk[1]
        tl = list(best.items())
        for n_ in new:
            n_.rd.extend(tl)

    def enter(name, new):
        old = region[name]
        import os
        if old is not new and not os.environ.get("K_NOINH"):
            inherit(new, old)
        region[name] = list(new)

    def wk_res(lo, hi):
        return [r_wk[i] for i in range(lo // 1024, (hi + 1023) // 1024)]

    def col(c):
        return COLS[:, c:c + 1]

    ctoks = [k.dma(sp, COLS[:, :], cols_in[:, :]),
             k.dma(sp, MROW[:, :], mrow_in[:, :]),
             k.dma(sp, IDN[:, :], ident_in[:, :]),
             k.op(dve, lambda e: e.memset(ONES[:, :], 1.0)),
             k.op(dve, lambda e: e.memset(ZC[:, :], 0.0)),
             k.op(dve, lambda e: e.memset(TLS[:, :, :], 0.0))]
    for e_ in (pe, act, dve):
        k._wait(e_, ctoks)
    xv = xT_in.rearrange("(c p) t -> p c t", p=128)
    for c4 in range(4):
        k.dma(sp, XT[:, c4 * 4:(c4 + 1) * 4, :], xv[:, c4 * 4:(c4 + 1) * 4, :],
              wr=[r for c in range(c4 * 4, c4 * 4 + 4) for r in r_xt[c]])

    class WStream:
        def __init__(self):
            self.plan = []
            self.issued = 0
            self.used = 0
            self.r_slot = [Res(), Res()]

        def add(self, src, shape, dep=None, slot=None):
            self.plan.append((src, shape, dep, slot))

        def _view(self, slot, shape):
            n = shape[0] * shape[1]
            if slot < 2:
                base = WS[:, slot, 0:n]
                res = [self.r_slot[slot]]
            else:
                base = ABC[:, slot - 2].rearrange("p c t -> p (c t)")[:, 0:n]
                res = list(r_abc[slot - 2])
            return base.rearrange("p (a b) -> p a b", a=shape[0]), res

        def _issue(self):
            i = self.issued
            src, shape, dep, slot = self.plan[i]
            dst, res = self._view(slot, shape)
            k.dma(pool, dst, src, rd=([dep] if dep is not None else []), wr=res)
            self.issued += 1

        def get(self):
            i = self.used
            while self.issued < len(self.plan):
                j = self.issued
                if j > i and (j > i + 3 or self.plan[j][3] in [self.plan[m][3] for m in range(i, j)]):
                    break
                self._issue()
            src, shape, dep, slot = self.plan[i]
            self.used += 1
            return self._view(slot, shape)

    ws = WStream()
    PROJ_ORDER = [OFF_K, OFF_K + 512, OFF_V, OFF_V + 512, OFF_XR, OFF_XR + 512, OFF_SCC, OFF_SCC + 512,
                  OFF_SCX, OFF_SCX + 512, OFF_Q, OFF_Q + 512, OFF_GR, OFF_GR + 512, OFF_SCB, OFF_SCB + 512]
    for l in range(depth):
        wi = wap("w_in", l).rearrange("(kc p) n -> p kc n", p=128)
        d_in = r_wg[("w_in", l)]
        for pi_, n0 in enumerate(PROJ_ORDER):
            ws.add(wi[:, :, n0:n0 + 512], (16, 512), d_in, slot=pi_ % 4)
        for sc in range(4):
            for br, wb in enumerate(("w_attn_o", "w_lru_o", "w_sc_o")):
                n0 = OFF_G + br * 2048 + sc * 512
                ws.add(wi[:, :, n0:n0 + 512], (16, 512), d_in)
                ws.add(wap(wb, l).rearrange("(kc p) n -> p kc n", p=128)[:, :, sc * 512:(sc + 1) * 512], (8, 512),
                       r_wg[(wb, l)])
            ws.add(wap("w_out", l).rearrange("(kc p) n -> p kc n", p=128)[:, sc * 4:(sc + 1) * 4, :], (4, 2048),
                   r_wg[("w_out", l)])
        wu = wap("w_mlp_up", l).rearrange("(kc p) n -> p kc n", p=128)
        wd = wap("w_mlp_down", l).rearrange("(kc p) n -> p kc n", p=128)
        for sp_ in range(8):
            for hh in range(2):
                n0 = sp_ * 1024 + hh * 512
                ws.add(wu[:, :, n0:n0 + 512], (16, 512), r_wg[("w_mlp_up", l)])
            for hh in range(2):
                ws.add(wd[:, sp_ * 8:(sp_ + 1) * 8, hh * 1024:(hh + 1) * 1024], (8, 1024), r_wg[("w_mlp_down", l)])

    _alt = 0
    for i_, (src_, shape_, dep_, slot_) in enumerate(ws.plan):
        if slot_ is None:
            ws.plan[i_] = (src_, shape_, dep_, _alt)
            _alt ^= 1
        else:
            _alt = 0
    bank_i = [0]

    def bank(ring=(0, 1, 2, 3, 4, 5, 6, 7)):
        b = ring[bank_i[0] % len(ring)]
        bank_i[0] += 1
        return b

    import os
    nb_ = [0]

    def rmsnorm(emit, extra=()):
        SQ = [WK[:, 0:2048].bitcast(F32), WK[:, 2048:4096].bitcast(F32)]
        r_sq = [Res(), Res()]
        RS = WK[:, 4096:6144].bitcast(F32)
        r_rs = Res()
        enter("WK", r_sq + [r_rs] + list(extra))
        nb_[0] += 1
        b0, b1 = (6, 7) if (nb_[0] % 2 or not os.environ.get("K_ALTB")) else (4, 5)
        for c in range(KC):
            s = c % 2
            k.op(act, lambda e: e.activation(out=SQ[s], in_=XT[:, c, :], func=AF.Square),
                 rd=r_xt[c], wr=[r_sq[s]])
            k.mm([lambda e, th=th: e.matmul(PS[:, (b0, b1)[th], :], lhsT=ONES[:, :],
                                            rhs=SQ[s][:, th * 512:(th + 1) * 512],
                                            start=(c == 0), stop=(c == KC - 1)) for th in range(2)],
                 rd=[r_sq[s], r_const], wr=[r_ps[b0], r_ps[b1]])
        for th, b in enumerate((b0, b1)):
            k.op(act, lambda e: e.activation(out=RS[:, th * 512:(th + 1) * 512], in_=PS[:, b, :],
                                             func=AF.Sqrt, scale=1.0 / D, bias=EPS),
                 rd=[r_ps[b]], wr=[r_rs])
        k.op(dve, lambda e: e.reciprocal(RS, RS), rd=[r_rs], wr=[r_rs])
        for c in range(KC):
            emit(c, RS, r_rs)

    def norm_to_ht(gcol0):
        enter("RX", r_ht)

        def emit(c, RS, r_rs):
            import os
            if os.environ.get("K_STT"):
                k.op(dve, lambda e: e.tensor_tensor(out=HT[:, c, :], in0=XT[:, c, :], in1=RS, op=ALU.mult),
                     rd=r_xt[c] + [r_rs, r_const], wr=[r_ht[c]])
                return
            k.op(dve, lambda e: e.scalar_tensor_tensor(out=HT[:, c, :], in0=XT[:, c, :], scalar=col(gcol0 + c),
                                                       in1=RS, op0=ALU.mult, op1=ALU.mult),
                 rd=r_xt[c] + [r_rs, r_const], wr=[r_ht[c]])
        rmsnorm(emit)

    def fm_groups(wt, r_w, kcn, nchunks, act_fn, act_res, sink):
        for ncn in range(nchunks):
            bb = [bank(), bank()]
            k.mm([lambda e, kc=kc, th=th: e.matmul(
                PS[:, bb[th], :], lhsT=wt[:, kc, ncn * 128:(ncn + 1) * 128],
                rhs=act_fn(kc)[:, th * 512:(th + 1) * 512], start=(kc == 0), stop=(kc == kcn - 1))
                for kc in range(kcn) for th in range(2)],
                rd=r_w + act_res, wr=[r_ps[bb[0]], r_ps[bb[1]]])
            for th in range(2):
                sink(ncn, th, bb[th])

    stage_i = [0]
    NSTG = 20
    STG = [WK[:, i * 512:(i + 1) * 512] for i in range(NSTG)]

    def evac(dst, b, wr):
        e = k.ev()
        if e is act:
            return k.op(act, lambda e_: e_.copy(out=dst, in_=PS[:, b, :]), rd=[r_ps[b]], wr=wr)
        return k.op(dve, lambda e_: e_.tensor_copy(out=dst, in_=PS[:, b, :]), rd=[r_ps[b]], wr=wr)

    def layer(l):
        cb = l * LC
        par = l % 2
        norm_to_ht(cb + C_G1)
        import os
        if os.environ.get("K_TWICE"):
            norm_to_ht(cb + C_G1)
        if not os.environ.get("K_NOHSP"):
            k.dma(sp, hsp.rearrange("(c p) t -> p c t", p=128), HT[:, :, :], rd=r_ht, wr=[r_hsp])
        if not os.environ.get("K_NOLW"):
            k.dma(pool, LW[:, 0, :, :], lru_wa[l].rearrange("g i j -> i g j"), wr=[r_lw])
            k.dma(pool, LW[:, 1, :, :], lru_wx[l].rearrange("g i j -> i g j"), wr=[r_lw])
        if STOP == "norm":
            return
        QT = ABC[:, 2]
        hres = list(r_ht)
        r_stg = [Res() for _ in range(NSTG)]
        enter("WK", r_stg)

        def stage(b):
            si = stage_i[0] % NSTG
            stage_i[0] += 1
            evac(STG[si], b, [r_stg[si]])
            return si

        def spill_sink(kind, pbase):
            def sink(ncn, th, b):
                ct = pbase + ncn
                si = stage(b)
                k.dma(sp, spill[kind, ct * 128:(ct + 1) * 128, th * 512:(th + 1) * 512], STG[si],
                      rd=[r_stg[si]], wr=[r_spill[kind][ct]])
                if th == 1:
                    if kind == SP_XR:
                        k.op(dve, lambda e_: e_.tensor_copy(out=TLS[:, ct, 0:3], in_=STG[si][:, 509:512]),
                             rd=[r_stg[si]], wr=[r_tls])
                    elif kind == SP_SCC:
                        k.op(dve, lambda e_: e_.tensor_copy(out=TLS[:, ct, 3:5], in_=STG[si][:, 510:512]),
                             rd=[r_stg[si]], wr=[r_tls])
                    elif kind == SP_SCX:
                        k.op(dve, lambda e_: e_.tensor_copy(out=TLS[:, ct, 5:7], in_=STG[si][:, 510:512]),
                             rd=[r_stg[si]], wr=[r_tls])
            return sink

        def k_sink(pbase):
            def sink(ncn, th, b):
                hd = pbase + ncn
                si = stage(b)
                g_ = hd // 4
                k.dma(sp, ck[par][g_].ap()[(hd % 4) * 128:(hd % 4 + 1) * 128, th * 512:(th + 1) * 512], STG[si],
                      rd=[r_stg[si]], wr=[r_ck[par][g_]])
                if hd % 4 == 3 and th == 1:
                    pending.append((ck[par][g_], ckg[par][g_], r_ck[par][g_], r_ckg[par][g_]))
            return sink

        def q_sink(pbase):
            def sink(ncn, th, b):
                hd = pbase + ncn
                evac(QT[:, hd, th * 512:(th + 1) * 512], b, [r_abc[2][hd]])
            return sink

        hfn = lambda kc: HT[:, kc, :]
        pending = []
        for pi, n0 in enumerate(PROJ_ORDER):
            if STOP.startswith("p") and STOP[1:].isdigit() and pi >= int(STOP[1:]):
                return
            wt, r_w = ws.get()
            if len(pending) > 1 or (pending and pi >= len(PROJ_ORDER) - 3):
                a_, b_, ra_, rb_ = pending.pop(0)
                k.allgather(a_, b_, rd=[ra_], wr=[rb_])
            if n0 in (OFF_V, OFF_V + 512):
                pv = (n0 - OFF_V) // 512
                for tt in range(8):
                    b = bank()
                    k.mm([lambda e, kc=kc: e.matmul(
                        PS[:, b, :], lhsT=HT[:, kc, tt * 128:(tt + 1) * 128], rhs=wt[:, kc, :],
                        start=(kc == 0), stop=(kc == KC - 1)) for kc in range(KC)],
                        rd=r_w + hres, wr=[r_ps[b]])
                    si = stage(b)
                    k.dma(sp, cv[par][pv].ap()[tt * 128:(tt + 1) * 128, :], STG[si],
                          rd=[r_stg[si]], wr=[r_cv[par][pv]])
                pending.append((cv[par][pv], cvg[par][pv], r_cv[par][pv], r_cvg[par][pv]))
                continue
            pbase = (n0 % 1024) // 128
            if n0 < OFF_K:
                sink = q_sink(pbase)
            elif n0 < OFF_V:
                sink = k_sink(pbase)
            else:
                kind = {OFF_XR: SP_XR, OFF_GR: SP_GR, OFF_SCB: SP_SCB, OFF_SCC: SP_SCC,
                        OFF_SCX: SP_SCX}[n0 - (n0 % 1024)]
                sink = spill_sink(kind, pbase)
            fm_groups(wt, r_w, KC, 4, hfn, hres, sink)
            if n0 == OFF_SCX + 512:
                k.dma(sp, ctl[par].ap().rearrange("r (q c) -> (r q) c", c=64),
                      TLS[:, :, :].rearrange("p a b -> p (a b)"), rd=[r_tls], wr=[r_ctl[par]])
                pending.append((ctl[par], ctg[par], r_ctl[par], r_ctg[par]))
        while pending:
            a_, b_, ra_, rb_ = pending.pop(0)
            k.allgather(a_, b_, rd=[ra_], wr=[rb_])
        if STOP == "proj":
            return
        attention(l, par, QT)
        if STOP == "attn":
            return
        lru_sc(l, par)
        if STOP == "lru":
            return
        merge(l)
        if STOP == "merge":
            return
        mlp(l)

    def attention(l, par, QT):
        KG = RX[:, 0:3072].rearrange("p (r t) -> p r t", r=3)
        VW = 130
        VG = RX[:, 3072:3072 + 24 * VW].rearrange("p (a d) -> p a d", a=24)
        KL = RX[:, 6192:7216]
        KH = RX[:, 7216:7344]
        VL = RX[:, 7344:7344 + 8 * VW].rearrange("p (a d) -> p a d", a=8)
        VH = RX[:, 8384:8384 + VW]
        KSB = RX[:, 8520:8536]
        PB = [RX[:, 9216 + i * 512: 9216 + (i + 1) * 512] for i in range(4)]
        PT = [RX[:, 11264 + i * 512: 11264 + (i + 1) * 512].rearrange("p (a q) -> p a q", a=4) for i in range(4)]
        AO = RX[:, 13312:13440]
        r_kg = [Res() for _ in range(3)]
        r_vg = [Res() for _ in range(3)]
        r_kl, r_vl, r_kh, r_vh, r_ksb = Res(), Res(), Res(), Res(), Res()
        r_pb = [Res() for _ in range(4)]
        r_pt = [Res() for _ in range(4)]
        r_ao = Res()
        enter("RX", r_kg + r_vg + [r_kl, r_vl, r_kh, r_vh, r_ksb, r_ao] + r_pb + r_pt)
        r_sm2 = [Res(), Res()]
        r_rs2 = [Res(), Res()]
        r_ksf = Res()
        TMP = [WK[:, i_ * 256:(i_ + 1) * 256].bitcast(F32) for i_ in range(4)]
        BDG = WK[:, 1024:2048].rearrange("p (h q) -> p h q", h=NH)
        BPV = WK[:, 2048:3072].rearrange("p (h q) -> p h q", h=NH)
        r_tmp = [Res() for _ in range(4)]
        r_bm = Res()
        enter("WK", r_tmp + [r_bm])
        k.dma(sp, BDG, bdiag_in.rearrange("p (h q) -> p h q", h=NH), wr=[r_bm])
        k.dma(sp, BPV, bprev_in.rearrange("p (h q) -> p h q", h=NH), wr=[r_bm])
        tmp_i = [0]
        pg_i = [0]
        k.op(dve, lambda e: e.memset(VG[:, :, 128:VW], 1.0), wr=r_vg)
        k.op(dve, lambda e: e.memset(VL[:, :, 128:VW], 1.0), wr=[r_vl])
        AH = int(_os.environ.get("K_AH", NH))
        AT = int(_os.environ.get("K_AT", 8))
        AST = int(_os.environ.get("K_ASTAGE", 9))
        for h in range(AH):
            hg, hl = h // 4, h % 4
            for r in range(3):
                k.dma(sp, KG[:, r, :], ckg[par][hg].ap()[r * 512 + hl * 128: r * 512 + (hl + 1) * 128, :],
                      rd=[r_ckg[par][hg]], wr=[r_kg[r]])
                k.dma(sp, VG[:, r * 8:(r + 1) * 8, 0:128],
                      cvg[par][hg].ap()[r * 1024:(r + 1) * 1024, hl * 128:(hl + 1) * 128]
                      .rearrange("(a p) d -> p a d", p=128),
                      rd=[r_cvg[par][hg]], wr=[r_vg[r]])
            k.dma(sp, KL, ck[par][hg].ap()[hl * 128:(hl + 1) * 128, :], rd=[r_ck[par][hg]], wr=[r_kl])
            k.dma(sp, VL[:, :, 0:128], cv[par][hg].ap()[:, hl * 128:(hl + 1) * 128].rearrange("(a p) d -> p a d", p=128),
                  rd=[r_cv[par][hg]], wr=[r_vl])
            k.op(dve, lambda e: e.tensor_scalar(out=KH, in0=KG[:, 0, 896:1024], scalar1=col(C_OHP), scalar2=None,
                                                op0=ALU.mult), rd=[r_kg[0]], wr=[r_kh])
            k.op(dve, lambda e: e.tensor_scalar(out=VH, in0=VG[:, 7, :], scalar1=col(C_OHP), scalar2=None,
                                                op0=ALU.mult), rd=[r_vg[0]], wr=[r_vh])
            for r in (1, 2):
                k.op(dve, lambda e: e.scalar_tensor_tensor(out=KH, in0=KG[:, r, 896:1024], scalar=col(C_OHP + r),
                                                           in1=KH, op0=ALU.mult, op1=ALU.add),
                     rd=[r_kg[r]], wr=[r_kh])
                k.op(dve, lambda e: e.scalar_tensor_tensor(out=VH, in0=VG[:, r * 8 + 7, :], scalar=col(C_OHP + r),
                                                           in1=VH, op0=ALU.mult, op1=ALU.add),
                     rd=[r_vg[r]], wr=[r_vh])
            KSF = SM[:, 456:472]
            k.op(dve, lambda e: e.tensor_reduce(out=KSF[:, 0:4], in_=KL.rearrange("p (b s) -> p b s", b=4),
                                                axis=AX.X, op=ALU.add), rd=[r_kl], wr=[r_ksf])
            k.op(dve, lambda e: e.tensor_reduce(out=KSF[:, 4:16], in_=RX[:, 0:3072].rearrange("p (b s) -> p b s", b=12),
                                                axis=AX.X, op=ALU.add), rd=r_kg, wr=[r_ksf])
            k.op(dve, lambda e: e.tensor_copy(out=KSB, in_=KSF), rd=[r_ksf], wr=[r_ksb])
            def sm_views(t):
                so = (t % 2) * 168
                return dict(GM=SM[:, so + 0:so + 16], TOP=SM[:, so + 16:so + 24], THR=SM[:, so + 24:so + 25],
                            SEL=SM[:, so + 32:so + 48], BFAR=SM[:, so + 48:so + 64], SBN=SM[:, so + 64:so + 80],
                            BH=SM[:, so + 80:so + 81], BK=SM[:, so + 81:so + 84], RSC=SM[:, so + 96:so + 160],
                            RSUM=SM[:, so + 160:so + 161], RINV=SM[:, so + 161:so + 162])

            def G(t):
                v_ = sm_views(t)
                GM, TOP, THR, SEL, BFAR, SBN, BH, BK = (v_[x] for x in ("GM", "TOP", "THR", "SEL", "BFAR", "SBN", "BH", "BK"))
                r_sm = r_sm2[t % 2]
                qs = QT[:, h, t * 128:(t + 1) * 128]
                k.mm([lambda e: e.matmul(PS[:, 7, 0:16], lhsT=qs, rhs=KSB, start=True, stop=True)],
                     rd=[r_abc[2][h], r_ksb], wr=[r_ps[7]])
                k.op(dve, lambda e: e.tensor_tensor(out=GM, in0=PS[:, 7, 0:16], in1=MROW[:, t * 16:(t + 1) * 16],
                                                    op=ALU.add), rd=[r_ps[7]], wr=[r_sm])
                k.op(dve, lambda e: e.max(out=TOP, in_=GM), wr=[r_sm])
                k.op(dve, lambda e: e.tensor_scalar_max(out=THR, in0=TOP[:, 2:3], scalar1=-1.0e29), wr=[r_sm])
                k.op(dve, lambda e: e.tensor_scalar(out=SEL, in0=GM, scalar1=THR, scalar2=1.0, op0=ALU.is_ge,
                                                    op1=ALU.subtract), wr=[r_sm])
                k.op(dve, lambda e: e.tensor_scalar(out=BFAR, in0=SEL, scalar1=1.0e30, scalar2=col(C_CFAR + h),
                                                    op0=ALU.mult, op1=ALU.add), wr=[r_sm])
                k.op(dve, lambda e: e.tensor_scalar(out=SBN, in0=SEL, scalar1=1.0e30, scalar2=None, op0=ALU.mult),
                     wr=[r_sm])
                if t == 0:
                    k.op(dve, lambda e: e.tensor_scalar(out=BH, in0=SBN[:, 7:8], scalar1=col(C_OHP), scalar2=col(C_HNONE),
                                                        op0=ALU.mult, op1=ALU.add), wr=[r_sm])
                    for r in (1, 2):
                        k.op(dve, lambda e: e.scalar_tensor_tensor(out=BH, in0=SBN[:, 7 + 4 * r:8 + 4 * r],
                                                                   scalar=col(C_OHP + r), in1=BH, op0=ALU.mult,
                                                                   op1=ALU.add), wr=[r_sm])
                    for r in range(3):
                        k.op(dve, lambda e: e.tensor_scalar(out=BK[:, r:r + 1], in0=BFAR[:, 7 + 4 * r:8 + 4 * r],
                                                            scalar1=col(C_HKILL + r), scalar2=None, op0=ALU.add),
                             wr=[r_sm])

            def tile_groups(t):
                v_ = sm_views(t)
                BFAR, SBN, BH, BK = v_["BFAR"], v_["SBN"], v_["BH"], v_["BK"]
                own_tiles = []
                for kt in range(t + 1):
                    if kt == t:
                        own_tiles.append((VL[:, kt, :], "diag", None, None))
                    elif kt == t - 1:
                        bl = kt // 2
                        if t % 2 == 1:
                            own_tiles.append((VL[:, kt, :], "prev", None, None))
                        else:
                            own_tiles.append((VL[:, kt, :], "prev", ("s", bl), SBN[:, bl:bl + 1]))
                    else:
                        own_tiles.append((VL[:, kt, :], "far", ("f", kt // 2), BFAR[:, kt // 2: kt // 2 + 1]))
                groups = []
                for g0 in range(0, t + 1, 4):
                    g1 = min(g0 + 4, t + 1)
                    groups.append((KL[:, g0 * 128:(g0 + 4) * 128], own_tiles[g0:g1], r_kl, r_vl))
                if t == 0:
                    groups.append((KH, [(VH, "prev", ("h", 0), BH)], r_kh, r_vh))
                for r in range(3):
                    for half in range(2):
                        tl = []
                        for i in range(4):
                            kt = half * 4 + i
                            n = 4 + 4 * r + kt // 2
                            if t == 0 and kt == 7:
                                tl.append((VG[:, r * 8 + kt, :], "far", ("k", r), BK[:, r:r + 1]))
                            else:
                                tl.append((VG[:, r * 8 + kt, :], "far", ("f", n), BFAR[:, n:n + 1]))
                        groups.append((KG[:, r, half * 512:(half + 1) * 512], tl, r_kg[r], r_vg[r]))
                return groups

            items = []
            tstate = {}
            for t in range(AT):
                gs = tile_groups(t)
                tstate[t] = dict(nex=0, ntiles=sum(len(g[1]) for g in gs), tile_no=0, last_exp=None, ngroups=len(gs))
                for gi, g in enumerate(gs):
                    items.append(dict(t=t, gi=gi, g=g))

            def A(it, idx):
                t = it["t"]
                ks, tl, rk_, rv_ = it["g"]
                v_ = sm_views(t)
                RSC = v_["RSC"]
                r_sm, r_rs = r_sm2[t % 2], r_rs2[t % 2]
                st = tstate[t]
                qs = QT[:, h, t * 128:(t + 1) * 128]
                n = len(tl)
                sbk = (0, 1, 2)[idx % 3]
                pi_ = idx % 4
                it["sbk"], it["pi"] = sbk, pi_
                nk_ = ks.shape[-1]
                k.mm([lambda e: e.matmul(PS[:, sbk, 0:nk_], lhsT=qs, rhs=ks, start=True, stop=True)],
                     rd=[r_abc[2][h], rk_], wr=[r_ps[sbk]])
                P = PB[pi_]
                tmp_of = {}
                for i in range(n):
                    vsrc, kind, bkey, bap = tl[i]
                    if kind != "far":
                        ti = tmp_i[0] % 4
                        tmp_i[0] += 1
                        tmp_of[i] = ti
                        bm = BPV if kind == "prev" else BDG
                        k.op(dve, lambda e: e.scalar_tensor_tensor(
                            out=TMP[ti], in0=PS[:, sbk, i * 128:(i + 1) * 128], scalar=SCALE, in1=bm[:, h, :],
                            op0=ALU.mult, op1=ALU.add), rd=[r_ps[sbk], r_bm], wr=[r_tmp[ti]])
                extra_rd = [r_tmp[ti_] for ti_ in set(tmp_of.values())]
                i = 0
                firstg = True
                last_exp = None
                while i < n:
                    vsrc, kind, bkey, bap = tl[i]
                    j = i + 1
                    wr_ = ([r_pb[pi_]] if firstg else [])
                    if kind == "far":
                        while j < n and tl[j][1] == "far" and tl[j][2] == bkey:
                            j += 1
                        last_exp = k.op(act, lambda e: e.activation(
                            out=P[:, i * 128:j * 128], in_=PS[:, sbk, i * 128:j * 128], func=AF.Exp,
                            bias=bap, scale=SCALE),
                            rd=[r_ps[sbk], r_sm] + extra_rd, wr=wr_)
                    else:
                        ti = tmp_of[i]
                        last_exp = k.op(act, lambda e: e.activation(
                            out=P[:, i * 128:(i + 1) * 128], in_=TMP[ti], func=AF.Exp,
                            bias=(bap if bap is not None else 0.0), scale=1.0),
                            rd=[r_tmp[ti], r_sm], wr=wr_)
                    firstg = False
                    st["nex"] += 1
                    i = j
                r_pb[pi_].w = last_exp
                r_pb[pi_].rd = []
                r_ps[sbk].rd.append(last_exp)
                st["last_exp"] = last_exp

            def B(it, idx):
                ks, tl, rk_, rv_ = it["g"]
                n = len(tl)
                pi_ = it["pi"]
                P = PB[pi_]
                tb = (3, 4)[idx % 2]
                PSB = PS[:, tb, :].bitcast(BF16)
                k.mm([lambda e, i=i: e.transpose(out=PSB[:, i * 128:(i + 1) * 128], in_=P[:, i * 128:(i + 1) * 128],
                                                 identity=IDN[:, :]) for i in range(n)],
                     rd=[r_pb[pi_]], wr=[r_ps[tb]])
                k.op(dve, lambda e: e.tensor_copy(out=PT[pi_][:, 0:n, :].rearrange("p a q -> p (a q)"),
                                                  in_=PSB[:, 0:n * 128]), rd=[r_ps[tb]], wr=[r_pt[pi_]])

            def C(it, idx):
                t = it["t"]
                ks, tl, rk_, rv_ = it["g"]
                n = len(tl)
                pi_ = it["pi"]
                st = tstate[t]
                ob = (5, 6)[t % 2]
                tn0 = st["tile_no"]
                k.mm([lambda e, i=i: e.matmul(PS[:, ob, 0:129], lhsT=PT[pi_][:, i, :], rhs=tl[i][0][:, 0:129],
                                              start=(tn0 + i == 0), stop=(tn0 + i == st["ntiles"] - 1))
                      for i in range(n)], rd=[r_pt[pi_], rv_], wr=[r_ps[ob]])
                st["tile_no"] += n
                if it["gi"] == st["ngroups"] - 1:
                    F(t)

            def F(t):
                v_ = sm_views(t)
                RSC, RSUM, RINV = v_["RSC"], v_["RSUM"], v_["RINV"]
                r_sm, r_rs = r_sm2[t % 2], r_rs2[t % 2]
                st = tstate[t]
                ob = (5, 6)[t % 2]
                k.op(dve, lambda e: e.reciprocal(RINV, PS[:, ob, 128:129]), rd=[r_ps[ob]], wr=[r_sm])
                k.op(act, lambda e: e.activation(out=AO, in_=PS[:, ob, 0:128], func=AF.Copy, scale=RINV),
                     rd=[r_ps[ob], r_sm], wr=[r_ao])
                PSB7 = PS[:, 7, :].bitcast(BF16)
                k.mm([lambda e: e.transpose(out=PSB7[:, 512:640], in_=AO, identity=IDN[:, :])],
                     rd=[r_ao], wr=[r_ps[7]])
                k.op(act, lambda e: e.copy(out=ABC[:, 0, h, t * 128:(t + 1) * 128], in_=PSB7[:, 512:640]),
                     rd=[r_ps[7]], wr=[r_abc[0][h]])

            if AT > 0:
                G(0)
            NI = len(items)
            DB_, DC_ = 2, 4
            for idx in range(NI + DC_):
                if idx < NI:
                    it = items[idx]
                    if it["gi"] == 0 and it["t"] + 1 < AT:
                        G(it["t"] + 1)
                    A(it, idx)
                if 0 <= idx - DB_ < NI:
                    B(items[idx - DB_], idx - DB_)
                if 0 <= idx - DC_ < NI:
                    C(items[idx - DC_], idx - DC_)

    def lru_sc(l, par):
        cb = l * LC
        FW = [RX[:, 8192 + i * 2048: 8192 + (i + 1) * 2048].bitcast(F32) for i in range(4)]
        r_fw = [Res() for _ in range(4)]
        P2 = RX[:, 0:8192].rearrange("p (c t) -> p c t", c=8)
        r_p2 = [Res() for _ in range(8)]
        enter("RX", r_fw + r_p2)
        XR = WK[:, 0:1027]
        XCB = WK[:, 1032:2056]
        GRB = WK[:, 2056:3080]
        GB = WK[:, 3080:4104]
        CX = WK[:, 4104:5130]
        SB_ = WK[:, 5136:6160]
        SCX_ = WK[:, 6160:7184]
        TG = WK[:, 7184:7184 + 4 * 64].rearrange("p (r c) -> p r c", r=4)
        Y = WK[:, 7448:9496].bitcast(F32)
        r_xr, r_xcb, r_grb, r_gb, r_cx, r_sb, r_scx, r_tg, r_y = [Res() for _ in range(9)]
        enter("WK", [r_xr, r_xcb, r_grb, r_gb, r_cx, r_sb, r_scx, r_tg, r_y])
        CCH = SM[:, 344:352]
        CAR = SM[:, 352:368]
        CG = SM[:, 368:432].rearrange("p (r c) -> p r c", r=4)
        HS = SM[:, 336:340]
        r_car = Res()
        r_cch = Res()
        r_hs = Res()
        r_cg = Res()
        for r in range(3):
            k.dma(sp, TG[:, r, :], ctg[par].ap()[r * 8:(r + 1) * 8, :].rearrange("b (q c) -> (b q) c", c=64),
                  rd=[r_ctg[par]], wr=[r_tg])
        TGv = TG.rearrange("p r (a b) -> p r a b", a=8)
        k.op(act, lambda e: e.activation(out=CCH, in_=COLS[:, cb + C_LAM: cb + C_LAM + 8], func=AF.Exp, scale=-1.0),
             rd=[r_const], wr=[r_cch])
        k.op(act, lambda e: e.activation(out=CCH, in_=CCH, func=AF.Ln, bias=1.0, scale=1.0), wr=[r_cch])
        k.op(dve, lambda e: e.tensor_scalar(out=CCH, in0=CCH, scalar1=-8.0, scalar2=None, op0=ALU.mult), wr=[r_cch])
        XC, RA, IU, MA = FW
        for ct in range(8):
            k.dma(sp, XR[:, 3:1027], spill[SP_XR, ct * 128:(ct + 1) * 128, :], rd=[r_spill[SP_XR][ct]], wr=[r_xr])
            k.dma(sp, GRB, spill[SP_GR, ct * 128:(ct + 1) * 128, :], rd=[r_spill[SP_GR][ct]], wr=[r_grb])
            k.op(dve, lambda e: e.tensor_scalar(out=XR[:, 0:3], in0=TGv[:, 0, ct, 0:3], scalar1=col(C_OHP), scalar2=None,
                                                op0=ALU.mult), rd=[r_tg, r_const], wr=[r_xr])
            for r in (1, 2):
                k.op(dve, lambda e: e.scalar_tensor_tensor(out=XR[:, 0:3], in0=TGv[:, r, ct, 0:3],
                                                           scalar=col(C_OHP + r), in1=XR[:, 0:3], op0=ALU.mult,
                                                           op1=ALU.add), rd=[r_tg, r_const], wr=[r_xr])
            k.op(dve, lambda e: e.tensor_scalar(out=XC, in0=XR[:, 0:1024], scalar1=col(cb + C_CW + ct * 4),
                                                scalar2=col(cb + C_CB + ct), op0=ALU.mult, op1=ALU.add),
                 rd=[r_xr, r_const], wr=[r_fw[0]])
            for kk in range(1, 4):
                k.op(dve, lambda e: e.scalar_tensor_tensor(out=XC, in0=XR[:, kk:kk + 1024],
                                                           scalar=col(cb + C_CW + ct * 4 + kk), in1=XC,
                                                           op0=ALU.mult, op1=ALU.add),
                     rd=[r_xr, r_const], wr=[r_fw[0]])
            k.op(act, lambda e: e.copy(out=XCB, in_=XC), rd=[r_fw[0]], wr=[r_xcb])
            bA = [bank() for _ in range(4)]
            for th in range(2):
                k.mm([lambda e: e.matmul(PS[:, bA[th], :], lhsT=LW[:, 0, ct, :], rhs=XCB[:, th * 512:(th + 1) * 512],
                                         start=True, stop=True)], rd=[r_lw, r_xcb], wr=[r_ps[bA[th]]])
                k.mm([lambda e: e.matmul(PS[:, bA[2 + th], :], lhsT=LW[:, 1, ct, :], rhs=XCB[:, th * 512:(th + 1) * 512],
                                         start=True, stop=True)], rd=[r_lw, r_xcb], wr=[r_ps[bA[2 + th]]])
            for th in range(2):
                k.op(act, lambda e: e.activation(out=RA[:, th * 512:(th + 1) * 512], in_=PS[:, bA[th], :],
                                                 func=AF.Sigmoid, bias=col(cb + C_BA + ct), scale=1.0),
                     rd=[r_ps[bA[th]], r_const], wr=[r_fw[1]])
                k.op(act, lambda e: e.activation(out=IU[:, th * 512:(th + 1) * 512], in_=PS[:, bA[2 + th], :],
                                                 func=AF.Sigmoid, bias=col(cb + C_BX + ct), scale=1.0),
                     rd=[r_ps[bA[2 + th]], r_const], wr=[r_fw[2]])
            k.op(act, lambda e: e.activation(out=RA, in_=RA, func=AF.Exp, scale=CCH[:, ct:ct + 1]),
                 rd=[r_cch], wr=[r_fw[1]])
            k.op(pool, lambda e: e.tensor_tensor(out=MA, in0=RA, in1=RA, op=ALU.mult), rd=[r_fw[1]], wr=[r_fw[3]])
            k.op(act, lambda e: e.activation(out=MA, in_=MA, func=AF.Sqrt, scale=-1.0, bias=1.0), wr=[r_fw[3]])
            k.op(pool, lambda e: e.tensor_tensor(out=IU, in0=IU, in1=MA, op=ALU.mult), rd=[r_fw[3]], wr=[r_fw[2]])
            k.op(pool, lambda e: e.tensor_tensor(out=IU, in0=IU, in1=XC, op=ALU.mult), rd=[r_fw[0]], wr=[r_fw[2]])
            k.dma(sp, SB_, spill[SP_SCB, ct * 128:(ct + 1) * 128, :], rd=[r_spill[SP_SCB][ct]], wr=[r_sb])
            k.dma(sp, CX[:, 2:1026], spill[SP_SCC, ct * 128:(ct + 1) * 128, :], rd=[r_spill[SP_SCC][ct]], wr=[r_cx])
            k.dma(sp, SCX_, spill[SP_SCX, ct * 128:(ct + 1) * 128, :], rd=[r_spill[SP_SCX][ct]], wr=[r_scx])
            k.op(pool, lambda e: e.tensor_tensor(out=CX[:, 2:1026], in0=CX[:, 2:1026], in1=SCX_, op=ALU.mult),
                 rd=[r_scx], wr=[r_cx])
            for r in range(3):
                k.op(dve, lambda e: e.tensor_tensor(out=HS[:, 0:2], in0=TGv[:, r, ct, 3:5], in1=TGv[:, r, ct, 5:7],
                                                    op=ALU.mult), rd=[r_tg], wr=[r_hs])
                if r == 0:
                    k.op(dve, lambda e: e.tensor_scalar(out=CX[:, 0:2], in0=HS[:, 0:2], scalar1=col(C_OHP), scalar2=None,
                                                        op0=ALU.mult), rd=[r_hs], wr=[r_cx])
                else:
                    k.op(dve, lambda e: e.scalar_tensor_tensor(out=CX[:, 0:2], in0=HS[:, 0:2], scalar=col(C_OHP + r),
                                                               in1=CX[:, 0:2], op0=ALU.mult, op1=ALU.add),
                         rd=[r_hs], wr=[r_cx])
            k.op(dve, lambda e: e.tensor_scalar(out=Y, in0=CX[:, 0:1024], scalar1=col(cb + C_SW + ct * 3), scalar2=None,
                                                op0=ALU.mult), rd=[r_cx], wr=[r_y])
            for kk in (1, 2):
                k.op(dve, lambda e: e.scalar_tensor_tensor(out=Y, in0=CX[:, kk:kk + 1024],
                                                           scalar=col(cb + C_SW + ct * 3 + kk), in1=Y,
                                                           op0=ALU.mult, op1=ALU.add),
                     rd=[r_cx], wr=[r_y])
            k.op(pool, lambda e: e.tensor_tensor(out=ABC[:, 2, ct, :], in0=Y, in1=SB_, op=ALU.mult),
                 rd=[r_y, r_sb], wr=[r_abc[2][ct]])
            k.op(dve, lambda e: e.tensor_tensor_scan(out=XC, data0=RA, data1=IU, initial=0.0, op0=ALU.mult, op1=ALU.add),
                 rd=[r_fw[1], r_fw[2]], wr=[r_fw[0]])
            k.op(dve, lambda e: e.tensor_tensor_scan(out=MA, data0=RA, data1=ZC[:, 0:1].to_broadcast([128, T]),
                                                     initial=1.0, op0=ALU.mult, op1=ALU.add),
                 rd=[r_fw[1]], wr=[r_fw[3]])
            k.op(act, lambda e: e.activation(out=GB, in_=GRB, func=AF.Gelu_apprx_tanh), rd=[r_grb], wr=[r_gb])
            k.op(pool, lambda e: e.tensor_tensor(out=ABC[:, 1, ct, :], in0=XC, in1=GB, op=ALU.mult),
                 rd=[r_fw[0], r_gb], wr=[r_abc[1][ct]])
            k.op(pool, lambda e: e.tensor_tensor(out=P2[:, ct, :], in0=MA, in1=GB, op=ALU.mult),
                 rd=[r_fw[3], r_gb], wr=[r_p2[ct]])
            k.op(act, lambda e: e.copy(out=CAR[:, ct:ct + 1], in_=MA[:, 1023:1024]), rd=[r_fw[3]], wr=[r_car])
            k.op(act, lambda e: e.copy(out=CAR[:, 8 + ct:9 + ct], in_=XC[:, 1023:1024]), rd=[r_fw[0]], wr=[r_car])
        k.dma(sp, cin2[par].ap()[:, :], CAR, rd=[r_car], wr=[r_cin2[par]])
        k.allgather(cin2[par], cout2[par], rd=[r_cin2[par]], wr=[r_cout2[par]])
        if l + 1 < depth:
            gather_weights(l + 1)
        k.dma(sp, CG, cout2[par].ap().rearrange("(r p) c -> p r c", p=128), rd=[r_cout2[par]], wr=[r_cg])
        ST = SM[:, 432:440]
        AP_ = SM[:, 440:448]
        HP_ = SM[:, 448:456]
        r_st = Res()
        k.op(dve, lambda e: e.memset(ST, 0.0), wr=[r_st])
        for r in range(3):
            k.op(dve, lambda e: e.tensor_scalar(out=AP_, in0=CG[:, r, 0:8], scalar1=col(C_MLT + r),
                                                scalar2=col(C_NML + r), op0=ALU.mult, op1=ALU.add),
                 rd=[r_cg, r_const], wr=[r_st])
            k.op(dve, lambda e: e.tensor_scalar(out=HP_, in0=CG[:, r, 8:16], scalar1=col(C_MLT + r), scalar2=None,
                                                op0=ALU.mult), rd=[r_cg, r_const], wr=[r_st])
            k.op(dve, lambda e: e.tensor_tensor(out=ST, in0=ST, in1=AP_, op=ALU.mult), wr=[r_st])
            k.op(dve, lambda e: e.tensor_tensor(out=ST, in0=ST, in1=HP_, op=ALU.add), wr=[r_st])
        for ct in range(8):
            k.op(dve, lambda e: e.scalar_tensor_tensor(out=ABC[:, 1, ct, :], in0=P2[:, ct, :], scalar=ST[:, ct:ct + 1],
                                                       in1=ABC[:, 1, ct, :], op0=ALU.mult, op1=ALU.add),
                 rd=[r_p2[ct], r_st], wr=[r_abc[1][ct]])

    def add_into_xt(n, th, b):
        xv_ = XT[:, n, th * 512:(th + 1) * 512]
        k.op(dve, lambda e: e.tensor_tensor(out=xv_, in0=PS[:, b, :], in1=xv_, op=ALU.add),
             rd=[r_ps[b]], wr=[r_xt[n][th]])

    def merge(l):
        cb = l * LC
        enter("RX", r_ht)
        k.dma(sp, HT[:, :, :], hsp.rearrange("(c p) t -> p c t", p=128), rd=[r_hsp], wr=list(r_ht))
        SG = WK[:, 0:4096].rearrange("p (c t) -> p c t", c=4)
        MT = WK[:, 4096:8192].rearrange("p (c t) -> p c t", c=4)
        TF = [WK[:, 8192 + i * 1024: 8192 + (i + 1) * 1024].bitcast(F32) for i in range(2)]
        r_sg = [[Res(), Res()] for _ in range(4)]
        r_mt = [[Res(), Res()] for _ in range(4)]
        r_tf = [Res(), Res()]
        enter("WK", [x for y in r_sg + r_mt for x in y] + r_tf)
        hres = list(r_ht)
        hfn = lambda kc: HT[:, kc, :]
        tfi = [0]
        for sc in range(4):
            for br in range(3):
                wt, r_w = ws.get()

                def gsink(ncn, th, b):
                    k.op(act, lambda e: e.activation(out=SG[:, ncn, th * 512:(th + 1) * 512], in_=PS[:, b, :],
                                                     func=AF.Sigmoid,
                                                     bias=col(cb + C_GB + br * 16 + sc * 4 + ncn), scale=1.0),
                         rd=[r_ps[b], r_const], wr=[r_sg[ncn][th]])
                fm_groups(wt, r_w, KC, 4, hfn, hres, gsink)
                wt, r_w = ws.get()

                def ysink(ncn, th, b):
                    mv = MT[:, ncn, th * 512:(th + 1) * 512]
                    sv = SG[:, ncn, th * 512:(th + 1) * 512]
                    if br == 0:
                        k.op(dve, lambda e: e.tensor_tensor(out=mv, in0=PS[:, b, :], in1=sv, op=ALU.mult),
                             rd=[r_ps[b], r_sg[ncn][th]], wr=[r_mt[ncn][th]])
                    else:
                        ti = tfi[0] % 2
                        tfi[0] += 1
                        k.op(dve, lambda e: e.tensor_tensor(out=TF[ti], in0=PS[:, b, :], in1=sv, op=ALU.mult),
                             rd=[r_ps[b], r_sg[ncn][th]], wr=[r_tf[ti]])
                        k.op(dve, lambda e: e.tensor_tensor(out=mv, in0=TF[ti], in1=mv, op=ALU.add),
                             rd=[r_tf[ti]], wr=[r_mt[ncn][th]])
                fm_groups(wt, r_w, 8, 4, lambda kc: ABC[:, br, kc, :], list(r_abc[br]), ysink)
            wt, r_w = ws.get()
            for n in range(16):
                bb = [bank(), bank()]
                k.mm([lambda e, kc=kc, th=th: e.matmul(PS[:, bb[th], :], lhsT=wt[:, kc, n * 128:(n + 1) * 128],
                                                       rhs=MT[:, kc, th * 512:(th + 1) * 512],
                                                       start=(kc == 0), stop=(kc == 3))
                      for kc in range(4) for th in range(2)],
                     rd=r_w + [r_mt[c][th] for c in range(4) for th in range(2)],
                     wr=[r_ps[bb[0]], r_ps[bb[1]]])
                for th in range(2):
                    add_into_xt(n, th, bb[th])

    def mlp(l):
        cb = l * LC
        norm_to_ht(cb + C_G2)
        RT = [WK[:, 6144 + i * 1024: 6144 + (i + 1) * 1024].bitcast(F32) for i in range(2)]
        r_rt = [Res(), Res()]
        enter("WK", r_rt)
        hres = list(r_ht)
        hfn = lambda kc: HT[:, kc, :]
        rti = [0]
        for sp_ in range(8):
            us = sp_ % 2
            for hh in range(2):
                wt, r_w = ws.get()

                def usink(ncn, th, b):
                    ti = rti[0] % 2
                    rti[0] += 1
                    uc = hh * 4 + ncn
                    k.op(act, lambda e: e.activation(out=RT[ti], in_=PS[:, b, :], func=AF.Relu),
                         rd=[r_ps[b]], wr=[r_rt[ti]])
                    k.op(dve, lambda e: e.tensor_tensor(out=ABC[:, us, uc, th * 512:(th + 1) * 512], in0=RT[ti],
                                                        in1=RT[ti], op=ALU.mult), rd=[r_rt[ti]], wr=[r_abc[us][uc]])
                fm_groups(wt, r_w, KC, 4, hfn, hres, usink)
            for hh in range(2):
                wt, r_w = ws.get()
                for n8 in range(8):
                    bb = [bank(), bank()]
                    k.mm([lambda e, kc=kc, th=th: e.matmul(PS[:, bb[th], :], lhsT=wt[:, kc, n8 * 128:(n8 + 1) * 128],
                                                           rhs=ABC[:, us, kc, th * 512:(th + 1) * 512],
                                                           start=(kc == 0), stop=(kc == 7))
                          for kc in range(8) for th in range(2)],
                         rd=r_w + list(r_abc[us]), wr=[r_ps[bb[0]], r_ps[bb[1]]])
                    for th in range(2):
                        add_into_xt(hh * 8 + n8, th, bb[th])

    import os
    STOP = os.environ.get("K_STOP", "")
    if DBG == "attn":
        QT = ABC[:, 2]
        k.dma(sp, QT, qT_dbg.rearrange("(h p) t -> p h t", p=128), wr=list(r_abc[2]))
        attention(0, 0, QT)
        dt_ = k.dma(sp, aT_dbg.rearrange("(h p) t -> p h t", p=128), ABC[:, 0], rd=list(r_abc[0]))
        k._wait(sp, [dt_])
        STOP = "none"
    gather_weights(0)
    for l in range(depth):
        if STOP == "none":
            break
        layer(l)
    OB = [WK[:, 6144 + i * 2048: 6144 + (i + 1) * 2048].bitcast(F32) for i in range(2)]
    r_ob = [Res(), Res()]
    out_toks = []

    def femit(c, RS, r_rs):
        s = c % 2
        k.op(dve, lambda e: e.scalar_tensor_tensor(out=OB[s], in0=XT[:, c, :], scalar=col(C_FG + c), in1=RS,
                                                   op0=ALU.mult, op1=ALU.mult),
             rd=r_xt[c] + [r_rs, r_const], wr=[r_ob[s]])
        out_toks.append(k.dma(sp, outT[c * 128:(c + 1) * 128, :], OB[s], rd=[r_ob[s]]))
    if os.environ.get("K_NOFINAL"):
        for c in range(KC):
            out_toks.append(k.dma(sp, outT[c * 128:(c + 1) * 128, :], XT[:, c, :], rd=r_xt[c]))
    else:
        rmsnorm(femit, extra=r_ob)
    k._wait(sp, out_toks)
    if k.cccnt:
        k._wait(sp, [(k.ccsem, k.cccnt)])
        k._wait(pool, [(k.ccsem, k.cccnt)])
    assert STOP or ws.used == len(ws.plan), (ws.used, len(ws.plan))
    return nc, es


def _t5_bucket(n):
    n = np.maximum(n, 0)
    max_exact = 16
    nf = np.maximum(n, 1).astype(np.float32)
    large = max_exact + (np.log(nf / np.float32(max_exact)) / np.float32(math.log(128 / max_exact))
                         * np.float32(32 - max_exact)).astype(np.int32)
    large = np.minimum(large, 31)
    return np.where(n < max_exact, n, large)


def _host_consts(inp, core):
    j = core % 4
    f32 = np.float32
    cols = np.zeros((128, NCOL), f32)

    def put(c0, vec):
        v = np.asarray(vec, f32).reshape(-1, 128).T
        cols[:, c0:c0 + v.shape[1]] = v

    for l in range(DEPTH):
        cb = l * LC
        put(cb + C_G1, inp["norm_mix_g"][l])
        put(cb + C_G2, inp["norm_mlp_g"][l])
        put(cb + C_GB, inp["gate_b"][l])
        cw = np.asarray(inp["lru_conv_w"][l], f32)
        cols[:, cb + C_CW: cb + C_CW + 32] = cw.reshape(4, 8, 128).transpose(2, 1, 0).reshape(128, 32)
        put(cb + C_CB, inp["lru_conv_b"][l])
        put(cb + C_BA, inp["lru_ba"][l])
        put(cb + C_BX, inp["lru_bx"][l])
        put(cb + C_LAM, inp["lru_lambda"][l])
        sw = np.asarray(inp["sc_conv_w"][l], f32)
        cols[:, cb + C_SW: cb + C_SW + 24] = sw.reshape(3, 8, 128).transpose(2, 1, 0).reshape(128, 24)
    put(C_FG, inp["final_g"])
    rel = np.asarray(inp["rel_table"], f32)
    cols[:, C_CFAR:C_CFAR + 8] = rel[31][None, :]
    for r in range(3):
        cols[:, C_OHP + r] = 1.0 if r == j - 1 else 0.0
        cols[:, C_HKILL + r] = NEG if r == j - 1 else 0.0
    for r in range(4):
        cols[:, C_MLT + r] = 1.0 if r < j else 0.0
        cols[:, C_NML + r] = 0.0 if r < j else 1.0
    cols[:, C_HNONE] = NEG if j == 0 else 0.0
    mrow = np.zeros((128, 128), f32)
    for t in range(8):
        for b in range(4):
            mrow[:, t * 16 + b] = 0.0 if b < t // 2 else NEG
        for n in range(12):
            mrow[:, t * 16 + 4 + n] = 0.0 if n < 4 * j else NEG
    ext = np.concatenate([rel, np.full((1, NH), NEG, f32)], axis=0)
    q = np.arange(128)[:, None]
    kk = np.arange(128)[None, :]
    idx_d = np.where(q >= kk, _t5_bucket(q - kk), 32)
    idx_p = _t5_bucket(128 + q - kk)
    bdiag = ext[idx_d].transpose(0, 2, 1).reshape(128, NH * 128).astype(ml_dtypes.bfloat16)
    bprev = ext[idx_p].transpose(0, 2, 1).reshape(128, NH * 128).astype(ml_dtypes.bfloat16)
    return cols, mrow, bdiag, bprev


_CACHE = {}


def kernel(**inp):
    depth = int(inp.pop("_depth", DEPTH))
    if depth not in _CACHE:
        _CACHE[depth] = build(depth)
    nc, es = _CACHE[depth]
    x = np.asarray(inp["x"], np.float32)
    shared = {}
    big = ("w_in", "w_attn_o", "w_lru_o", "w_sc_o", "w_out", "w_mlp_up", "w_mlp_down")
    for name in big + ("lru_wa", "lru_wx"):
        shared[name] = np.asarray(inp[name], np.float32)[:depth]
    ident = np.eye(128, dtype=np.float32).astype(ml_dtypes.bfloat16)
    in_maps = []
    for c in range(8):
        b, j = c // 4, c % 4
        cols, mrow, bdiag, bprev = _host_consts(inp, c)
        m = {}
        for name, w_ in shared.items():
            if GATHER and name in big:
                rr = w_.shape[1] // 8
                m[name] = np.ascontiguousarray(w_[:, c * rr:(c + 1) * rr, :])
            else:
                m[name] = np.ascontiguousarray(w_)
        m["xT"] = np.ascontiguousarray(x[b, j * T:(j + 1) * T, :].T)
        m["cols"] = cols
        m["mrow"] = mrow
        m["bdiag"] = bdiag
        m["bprev"] = bprev
        m["ident"] = ident
        in_maps.append(m)
    res = run_bass_kernel_spmd(nc, in_maps, core_ids=list(range(8)))
    out = np.empty((2, 4 * T, D), np.float32)
    for c in range(8):
        b, j = c // 4, c % 4
        out[b, j * T:(j + 1) * T, :] = res.results[c]["outT"].T
    return out
```

```python
import math
from contextlib import ExitStack
import numpy as np
import ml_dtypes
import concourse.bass as bass
import concourse.mybir as mybir
from concourse.bass_utils import run_bass_kernel_spmd

F32, BF16 = mybir.dt.float32, mybir.dt.bfloat16
AF = mybir.ActivationFunctionType
ALU = mybir.AluOpType
AX = mybir.AxisListType

D = 2048
KC = 16
T = 1024
NIN = 14336
DEPTH = 4
NH = 8
DFF = 8192
EPS = 1e-6
NEG = -1.0e30
RB = 2056
SCALE = 128 ** -0.5

OFF_Q, OFF_K, OFF_V, OFF_XR, OFF_GR, OFF_SCB, OFF_SCC, OFF_SCX, OFF_G = (
    0, 1024, 2048, 3072, 4096, 5120, 6144, 7168, 8192)

LC = 168
C_G1, C_G2, C_GB, C_CW, C_CB, C_BA, C_BX, C_LAM, C_SW = 0, 16, 32, 80, 112, 120, 128, 136, 144
C_FG = DEPTH * LC
C_CFAR = C_FG + 16
C_OHP = C_CFAR + 8
C_MLT = C_OHP + 3
C_NML = C_MLT + 4
C_HKILL = C_NML + 4
C_HNONE = C_HKILL + 3
NCOL = C_HNONE + 1 + 3


class Res:
    __slots__ = ("w", "rd", "const")

    def __init__(self, const=False):
        self.w = None
        self.rd = []
        self.const = const


class Eng:
    def __init__(self, obj, sem, name):
        self.obj = obj
        self.sem = sem
        self.cnt = 0
        self.seen = {}
        self.name = name
        self.dsems = []
        self.dcnt = []
        self.dnext = 0


class K:
    def __init__(self, nc, es, depth):
        self.nc = nc
        self.es = es
        self.depth = depth
        mk = lambda n: es.enter_context(nc.semaphore(n))
        self.pe = Eng(nc.tensor, mk("s_pe"), "pe")
        self.act = Eng(nc.scalar, mk("s_act"), "act")
        self.dve = Eng(nc.vector, mk("s_dve"), "dve")
        self.pool = Eng(nc.gpsimd, mk("s_pool"), "pool")
        self.sp = Eng(nc.sync, mk("s_sp"), "sp")
        for q, n in ((self.sp, 24), (self.pool, 4)):
            for i in range(n):
                q.dsems.append(mk(f"d_{q.name}{i}"))
                q.dcnt.append(0)
        self.ccsem = mk("s_cc")
        self.cccnt = 0
        self.flip = 0

    def _need(self, rd, wr):
        toks = []
        for r in rd:
            if r.w is not None:
                toks.append(r.w)
        for w in wr:
            if w.w is not None:
                toks.append(w.w)
            toks.extend(w.rd)
        return toks

    def _wait(self, e, toks):
        best = {}
        for (sem, val) in toks:
            if e.seen.get(sem, 0) < val and best.get(sem, 0) < val:
                best[sem] = val
        for sem, val in best.items():
            if sem is e.sem and e is self.pe:
                continue
            e.obj.wait_ge(sem, val)
            e.seen[sem] = val

    def _commit(self, tok, rd, wr):
        for r in rd:
            if not r.const:
                r.rd.append(tok)
        for w in wr:
            w.w = tok
            w.rd = []

    def op(self, e, fn, rd=(), wr=()):
        self._wait(e, self._need(rd, wr))
        ins = fn(e.obj)
        e.cnt += 1
        ins.then_inc(e.sem, 1)
        tok = (e.sem, e.cnt)
        self._commit(tok, rd, wr)
        return tok

    def mm(self, mms, rd=(), wr=()):
        e = self.pe
        self._wait(e, self._need(rd, wr))
        ins = None
        for f in mms:
            ins = f(e.obj)
        e.cnt += 1
        ins.then_inc(e.sem, 1)
        tok = (e.sem, e.cnt)
        self._commit(tok, rd, wr)
        return tok

    def dma(self, q, out, in_, rd=(), wr=(), **kw):
        i = q.dnext
        q.dnext = (q.dnext + 1) % len(q.dsems)
        sem = q.dsems[i]
        toks = self._need(rd, wr)
        if q.dcnt[i] > 0:
            toks.append((sem, 16 * q.dcnt[i]))
        self._wait(q, toks)
        q.dcnt[i] += 1
        q.obj.dma_start(out=out, in_=in_, **kw).then_inc(sem, 16)
        tok = (sem, 16 * q.dcnt[i])
        self._commit(tok, rd, wr)
        return tok

    def allgather(self, cin, cout, rd=(), wr=()):
        q = self.pool
        self._wait(q, self._need(rd, wr))
        self.cccnt += 1
        q.obj.collective_compute(
            "AllGather", ALU.bypass, replica_groups=[[0, 1, 2, 3], [4, 5, 6, 7]],
            ins=[cin.ap().opt()], outs=[cout.ap().opt()]).then_inc(self.ccsem)
        tok = (self.ccsem, self.cccnt)
        self._commit(tok, rd, wr)
        return tok

    def ev(self):
        self.flip ^= 1
        return self.act if self.flip else self.dve


import os as _os
GATHER = bool(_os.environ.get('K_GATHER'))


def build(depth=DEPTH):
    nc = bass.Bass("TRN2", target_bir_lowering=False)
    es = ExitStack()
    k = K(nc, es, depth)
    pe, act, dve, pool, sp = k.pe, k.act, k.dve, k.pool, k.sp

    def din(name, shape, dt=F32):
        return nc.dram_tensor(name, list(shape), dt, kind="ExternalInput").ap()

    xT_in = din("xT", [D, T])
    WSHAPES = {"w_in": (D, NIN), "w_attn_o": (1024, D), "w_lru_o": (1024, D), "w_sc_o": (1024, D),
               "w_out": (D, D), "w_mlp_up": (D, DFF), "w_mlp_down": (DFF, D)}
    wsrc = {}
    r_wg = {}
    wshard = {}
    for name, (R_, C_) in WSHAPES.items():
        if GATHER:
            wshard[name] = din(name, [depth, R_ // 8, C_])
            wsrc[name] = []
            for l in range(depth):
                g_ = nc.dram_tensor(f"{name}_g{l}", [R_, C_], F32)
                wsrc[name].append(g_)
                r_wg[(name, l)] = Res()
        else:
            full = din(name, [depth, R_, C_])
            wsrc[name] = [full[l] for l in range(depth)]
            r_wg.update({(name, l): Res() for l in range(depth)})
    lru_wa = din("lru_wa", [depth, 8, 128, 128])
    lru_wx = din("lru_wx", [depth, 8, 128, 128])

    def wap(name, l):
        t_ = wsrc[name][l]
        return t_.ap() if GATHER else t_

    def gather_weights(l):
        if not GATHER:
            return
        for name, (R_, C_) in WSHAPES.items():
            bnc = nc.dram_tensor(f"{name}_b{l}", [R_ // 8, C_], F32)
            rb = Res()
            k.dma(sp, bnc.ap()[:, :], wshard[name][l], wr=[rb])
            q = k.pool
            k._wait(q, k._need([rb], [r_wg[(name, l)]]))
            k.cccnt += 1
            q.obj.collective_compute("AllGather", ALU.bypass, replica_groups=[list(range(8))],
                                     ins=[bnc.ap().opt()], outs=[wsrc[name][l].ap().opt()]).then_inc(k.ccsem)
            tok = (k.ccsem, k.cccnt)
            k._commit(tok, [rb], [r_wg[(name, l)]])
    cols_in = din("cols", [128, NCOL])
    mrow_in = din("mrow", [128, 128])
    bdiag_in = din("bdiag", [128, NH * 128], BF16)
    bprev_in = din("bprev", [128, NH * 128], BF16)
    ident_in = din("ident", [128, 128], BF16)
    outT = nc.dram_tensor("outT", [D, T], F32, kind="ExternalOutput").ap()

    DBG = _os.environ.get("K_DBG", "")
    dk = dict(kind="ExternalInput") if DBG == "attn" else {}
    ck = [[nc.dram_tensor(f"ck{i}_{g}", [512, 1024], BF16, **(dk if i == 0 else {})) for g in range(2)] for i in range(2)]
    ckg = [[nc.dram_tensor(f"ckg{i}_{g}", [4 * 512, 1024], BF16, **(dk if i == 0 else {})) for g in range(2)] for i in range(2)]
    cv = [[nc.dram_tensor(f"cv{i}_{g}", [1024, 512], BF16, **(dk if i == 0 else {})) for g in range(2)] for i in range(2)]
    cvg = [[nc.dram_tensor(f"cvg{i}_{g}", [4 * 1024, 512], BF16, **(dk if i == 0 else {})) for g in range(2)] for i in range(2)]
    if DBG == "attn":
        qT_dbg = nc.dram_tensor("qT_dbg", [1024, 1024], BF16, kind="ExternalInput").ap()
        aT_dbg = nc.dram_tensor("aT_dbg", [1024, 1024], BF16, kind="ExternalOutput").ap()
    ctl = [nc.dram_tensor(f"ctl{i}", [8, 1024], BF16) for i in range(2)]
    ctg = [nc.dram_tensor(f"ctg{i}", [32, 1024], BF16) for i in range(2)]
    r_ck = [[Res(), Res()] for _ in range(2)]
    r_ckg = [[Res(), Res()] for _ in range(2)]
    r_cv = [[Res(), Res()] for _ in range(2)]
    r_cvg = [[Res(), Res()] for _ in range(2)]
    r_ctl = [Res(), Res()]
    r_ctg = [Res(), Res()]
    cin2 = [nc.dram_tensor(f"cinb{i}", [128, 16], F32) for i in range(2)]
    cout2 = [nc.dram_tensor(f"coutb{i}", [512, 16], F32) for i in range(2)]
    spill = nc.dram_tensor("spill", [5, 1024, 1024], BF16).ap()
    hsp = nc.dram_tensor("hsp", [D, T], BF16).ap()
    r_cin2 = [Res(), Res()]
    r_cout2 = [Res(), Res()]
    r_spill = [[Res() for _ in range(8)] for _ in range(5)]
    r_hsp = Res()
    SP_XR, SP_GR, SP_SCB, SP_SCC, SP_SCX = 0, 1, 2, 3, 4

    sb = lambda name, shape, dt: es.enter_context(nc.sbuf_tensor(name, list(shape), dt))
    XT = sb("XT", [128, KC, T], F32)
    r_xt = [[Res(), Res()] for _ in range(KC)]
    RX = sb("RX", [128, 16384], BF16)
    ABC = sb("ABC", [128, 3, 8, T], BF16)
    WS = sb("WS", [128, 2, 8192], BF16)
    WK = sb("WK", [128, 10240], BF16)
    COLS = sb("COLS", [128, NCOL], F32)
    MROW = sb("MROW", [128, 128], F32)
    IDN = sb("IDN", [128, 128], BF16)
    ONES = sb("ONES", [128, 128], F32)
    ZC = sb("ZC", [128, 2], F32)
    LW = sb("LW", [128, 2, 8, 128], BF16)
    SM = sb("SM", [128, 480], F32)
    TLS = sb("TLS", [128, 8, 8], BF16)
    r_const = Res(const=True)
    r_lw = Res()
    r_tls = Res()
    PS = es.enter_context(nc.psum_tensor("PS", [128, 8, 512], F32))
    r_ps = [Res() for _ in range(8)]

    HT = RX[:, :].rearrange("p (c t) -> p c t", c=KC)
    r_ht = [Res() for _ in range(KC)]

    def rx_res(lo, hi):
        return [r_ht[i] for i in range(lo // 1024, (hi + 1023) // 1024)]

    r_abc = [[Res() for _ in range(8)] for _ in range(3)]
    r_wk = [Res() for _ in range(10)]
    region = {"WK": list(r_wk), "RX": list(r_ht)}

    def inherit(new, old):
        best = {}
        for o in old:
            for tk in ([o.w] if o.w is not None else []) + o.rd:
                if best.get(tk[0], 0) < tk[1]:
                    best[tk[0]] = tk[1]
        tl = list(best.items())
        for n_ in new:
            n_.rd.extend(tl)

    def enter(name, new):
        old = region[name]
        import os
        if old is not new and not os.environ.get("K_NOINH"):
            inherit(new, old)
        region[name] = list(new)

    def wk_res(lo, hi):
        return [r_wk[i] for i in range(lo // 1024, (hi + 1023) // 1024)]

    def col(c):
        return COLS[:, c:c + 1]

    ctoks = [k.dma(sp, COLS[:, :], cols_in[:, :]),
             k.dma(sp, MROW[:, :], mrow_in[:, :]),
             k.dma(sp, IDN[:, :], ident_in[:, :]),
             k.op(dve, lambda e: e.memset(ONES[:, :], 1.0)),
             k.op(dve, lambda e: e.memset(ZC[:, :], 0.0)),
             k.op(dve, lambda e: e.memset(TLS[:, :, :], 0.0))]
    for e_ in (pe, act, dve):
        k._wait(e_, ctoks)
    xv = xT_in.rearrange("(c p) t -> p c t", p=128)
    for c4 in range(4):
        k.dma(sp, XT[:, c4 * 4:(c4 + 1) * 4, :], xv[:, c4 * 4:(c4 + 1) * 4, :],
              wr=[r for c in range(c4 * 4, c4 * 4 + 4) for r in r_xt[c]])

    class WStream:
        def __init__(self):
            self.plan = []
            self.issued = 0
            self.used = 0
            self.r_slot = [Res(), Res()]

        def add(self, src, shape, dep=None, slot=None):
            self.plan.append((src, shape, dep, slot))

        def _view(self, slot, shape):
            n = shape[0] * shape[1]
            if slot < 2:
                base = WS[:, slot, 0:n]
                res = [self.r_slot[slot]]
            else:
                base = ABC[:, slot - 2].rearrange("p c t -> p (c t)")[:, 0:n]
                res = list(r_abc[slot - 2])
            return base.rearrange("p (a b) -> p a b", a=shape[0]), res

        def _issue(self):
            i = self.issued
            src, shape, dep, slot = self.plan[i]
            dst, res = self._view(slot, shape)
            k.dma(pool, dst, src, rd=([dep] if dep is not None else []), wr=res)
            self.issued += 1

        def get(self):
            i = self.used
            while self.issued < len(self.plan):
                j = self.issued
                if j > i and (j > i + 3 or self.plan[j][3] in [self.plan[m][3] for m in range(i, j)]):
                    break
                self._issue()
            src, shape, dep, slot = self.plan[i]
            self.used += 1
            return self._view(slot, shape)

    ws = WStream()
    PROJ_ORDER = [OFF_K, OFF_K + 512, OFF_V, OFF_V + 512, OFF_XR, OFF_XR + 512, OFF_SCC, OFF_SCC + 512,
                  OFF_SCX, OFF_SCX + 512, OFF_Q, OFF_Q + 512, OFF_GR, OFF_GR + 512, OFF_SCB, OFF_SCB + 512]
    for l in range(depth):
        wi = wap("w_in", l).rearrange("(kc p) n -> p kc n", p=128)
        d_in = r_wg[("w_in", l)]
        for pi_, n0 in enumerate(PROJ_ORDER):
            ws.add(wi[:, :, n0:n0 + 512], (16, 512), d_in, slot=pi_ % 4)
        for sc in range(4):
            for br, wb in enumerate(("w_attn_o", "w_lru_o", "w_sc_o")):
                n0 = OFF_G + br * 2048 + sc * 512
                ws.add(wi[:, :, n0:n0 + 512], (16, 512), d_in)
                ws.add(wap(wb, l).rearrange("(kc p) n -> p kc n", p=128)[:, :, sc * 512:(sc + 1) * 512], (8, 512),
                       r_wg[(wb, l)])
            ws.add(wap("w_out", l).rearrange("(kc p) n -> p kc n", p=128)[:, sc * 4:(sc + 1) * 4, :], (4, 2048),
                   r_wg[("w_out", l)])
        wu = wap("w_mlp_up", l).rearrange("(kc p) n -> p kc n", p=128)
        wd = wap("w_mlp_down", l).rearrange("(kc p) n -> p kc n", p=128)
        for sp_ in range(8):
            for hh in range(2):
                n0 = sp_ * 1024 + hh * 512
                ws.add(wu[:, :, n0:n0 + 512], (16, 512), r_wg[("w_mlp_up", l)])
            for hh in range(2):
                ws.add(wd[:, sp_ * 8:(sp_ + 1) * 8, hh * 1024:(hh + 1) * 1024], (8, 1024), r_wg[("w_mlp_down", l)])

    _alt = 0
    for i_, (src_, shape_, dep_, slot_) in enumerate(ws.plan):
        if slot_ is None:
            ws.plan[i_] = (src_, shape_, dep_, _alt)
            _alt ^= 1
        else:
            _alt = 0
    bank_i = [0]

    def bank(ring=(0, 1, 2, 3, 4, 5, 6, 7)):
        b = ring[bank_i[0] % len(ring)]
        bank_i[0] += 1
        return b

    import os
    nb_ = [0]

    def rmsnorm(emit, extra=()):
        SQ = [WK[:, 0:2048].bitcast(F32), WK[:, 2048:4096].bitcast(F32)]
        r_sq = [Res(), Res()]
        RS = WK[:, 4096:6144].bitcast(F32)
        r_rs = Res()
        enter("WK", r_sq + [r_rs] + list(extra))
        nb_[0] += 1
        b0, b1 = (6, 7) if (nb_[0] % 2 or not os.environ.get("K_ALTB")) else (4, 5)
        for c in range(KC):
            s = c % 2
            k.op(act, lambda e: e.activation(out=SQ[s], in_=XT[:, c, :], func=AF.Square),
                 rd=r_xt[c], wr=[r_sq[s]])
            k.mm([lambda e, th=th: e.matmul(PS[:, (b0, b1)[th], :], lhsT=ONES[:, :],
                                            rhs=SQ[s][:, th * 512:(th + 1) * 512],
                                            start=(c == 0), stop=(c == KC - 1)) for th in range(2)],
                 rd=[r_sq[s], r_const], wr=[r_ps[b0], r_ps[b1]])
        for th, b in enumerate((b0, b1)):
            k.op(act, lambda e: e.activation(out=RS[:, th * 512:(th + 1) * 512], in_=PS[:, b, :],
                                             func=AF.Sqrt, scale=1.0 / D, bias=EPS),
                 rd=[r_ps[b]], wr=[r_rs])
        k.op(dve, lambda e: e.reciprocal(RS, RS), rd=[r_rs], wr=[r_rs])
        for c in range(KC):
            emit(c, RS, r_rs)

    def norm_to_ht(gcol0):
        enter("RX", r_ht)

        def emit(c, RS, r_rs):
            import os
            if os.environ.get("K_STT"):
                k.op(dve, lambda e: e.tensor_tensor(out=HT[:, c, :], in0=XT[:, c, :], in1=RS, op=ALU.mult),
                     rd=r_xt[c] + [r_rs, r_const], wr=[r_ht[c]])
                return
            k.op(dve, lambda e: e.scalar_tensor_tensor(out=HT[:, c, :], in0=XT[:, c, :], scalar=col(gcol0 + c),
                                                       in1=RS, op0=ALU.mult, op1=ALU.mult),
                 rd=r_xt[c] + [r_rs, r_const], wr=[r_ht[c]])
        rmsnorm(emit)

    def fm_groups(wt, r_w, kcn, nchunks, act_fn, act_res, sink):
        for ncn in range(nchunks):
            bb = [bank(), bank()]
            k.mm([lambda e, kc=kc, th=th: e.matmul(
                PS[:, bb[th], :], lhsT=wt[:, kc, ncn * 128:(ncn + 1) * 128],
                rhs=act_fn(kc)[:, th * 512:(th + 1) * 512], start=(kc == 0), stop=(kc == kcn - 1))
                for kc in range(kcn) for th in range(2)],
                rd=r_w + act_res, wr=[r_ps[bb[0]], r_ps[bb[1]]])
            for th in range(2):
                sink(ncn, th, bb[th])

    stage_i = [0]
    NSTG = 20
    STG = [WK[:, i * 512:(i + 1) * 512] for i in range(NSTG)]

    def evac(dst, b, wr):
        e = k.ev()
        if e is act:
            return k.op(act, lambda e_: e_.copy(out=dst, in_=PS[:, b, :]), rd=[r_ps[b]], wr=wr)
        return k.op(dve, lambda e_: e_.tensor_copy(out=dst, in_=PS[:, b, :]), rd=[r_ps[b]], wr=wr)

    def layer(l):
        cb = l * LC
        par = l % 2
        norm_to_ht(cb + C_G1)
        import os
        if os.environ.get("K_TWICE"):
            norm_to_ht(cb + C_G1)
        if not os.environ.get("K_NOHSP"):
            k.dma(sp, hsp.rearrange("(c p) t -> p c t", p=128), HT[:, :, :], rd=r_ht, wr=[r_hsp])
        if not os.environ.get("K_NOLW"):
            k.dma(pool, LW[:, 0, :, :], lru_wa[l].rearrange("g i j -> i g j"), wr=[r_lw])
            k.dma(pool, LW[:, 1, :, :], lru_wx[l].rearrange("g i j -> i g j"), wr=[r_lw])
        if STOP == "norm":
            return
        QT = ABC[:, 2]
        hres = list(r_ht)
        r_stg = [Res() for _ in range(NSTG)]
        enter("WK", r_stg)

        def stage(b):
            si = stage_i[0] % NSTG
            stage_i[0] += 1
            evac(STG[si], b, [r_stg[si]])
            return si

        def spill_sink(kind, pbase):
            def sink(ncn, th, b):
                ct = pbase + ncn
                si = stage(b)
                k.dma(sp, spill[kind, ct * 128:(ct + 1) * 128, th * 512:(th + 1) * 512], STG[si],
                      rd=[r_stg[si]], wr=[r_spill[kind][ct]])
                if th == 1:
                    if kind == SP_XR:
                        k.op(dve, lambda e_: e_.tensor_copy(out=TLS[:, ct, 0:3], in_=STG[si][:, 509:512]),
                             rd=[r_stg[si]], wr=[r_tls])
                    elif kind == SP_SCC:
                        k.op(dve, lambda e_: e_.tensor_copy(out=TLS[:, ct, 3:5], in_=STG[si][:, 510:512]),
                             rd=[r_stg[si]], wr=[r_tls])
                    elif kind == SP_SCX:
                        k.op(dve, lambda e_: e_.tensor_copy(out=TLS[:, ct, 5:7], in_=STG[si][:, 510:512]),
                             rd=[r_stg[si]], wr=[r_tls])
            return sink

        def k_sink(pbase):
            def sink(ncn, th, b):
                hd = pbase + ncn
                si = stage(b)
                g_ = hd // 4
                k.dma(sp, ck[par][g_].ap()[(hd % 4) * 128:(hd % 4 + 1) * 128, th * 512:(th + 1) * 512], STG[si],
                      rd=[r_stg[si]], wr=[r_ck[par][g_]])
                if hd % 4 == 3 and th == 1:
                    pending.append((ck[par][g_], ckg[par][g_], r_ck[par][g_], r_ckg[par][g_]))
            return sink

        def q_sink(pbase):
            def sink(ncn, th, b):
                hd = pbase + ncn
                evac(QT[:, hd, th * 512:(th + 1) * 512], b, [r_abc[2][hd]])
            return sink

        hfn = lambda kc: HT[:, kc, :]
        pending = []
        for pi, n0 in enumerate(PROJ_ORDER):
            if STOP.startswith("p") and STOP[1:].isdigit() and pi >= int(STOP[1:]):
                return
            wt, r_w = ws.get()
            if len(pending) > 1 or (pending and pi >= len(PROJ_ORDER) - 3):
                a_, b_, ra_, rb_ = pending.pop(0)
                k.allgather(a_, b_, rd=[ra_], wr=[rb_])
            if n0 in (OFF_V, OFF_V + 512):
                pv = (n0 - OFF_V) // 512
                for tt in range(8):
                    b = bank()
                    k.mm([lambda e, kc=kc: e.matmul(
                        PS[:, b, :], lhsT=HT[:, kc, tt * 128:(tt + 1) * 128], rhs=wt[:, kc, :],
                        start=(kc == 0), stop=(kc == KC - 1)) for kc in range(KC)],
                        rd=r_w + hres, wr=[r_ps[b]])
                    si = stage(b)
                    k.dma(sp, cv[par][pv].ap()[tt * 128:(tt + 1) * 128, :], STG[si],
                          rd=[r_stg[si]], wr=[r_cv[par][pv]])
                pending.append((cv[par][pv], cvg[par][pv], r_cv[par][pv], r_cvg[par][pv]))
                continue
            pbase = (n0 % 1024) // 128
            if n0 < OFF_K:
                sink = q_sink(pbase)
            elif n0 < OFF_V:
                sink = k_sink(pbase)
            else:
                kind = {OFF_XR: SP_XR, OFF_GR: SP_GR, OFF_SCB: SP_SCB, OFF_SCC: SP_SCC,
                        OFF_SCX: SP_SCX}[n0 - (n0 % 1024)]
                sink = spill_sink(kind, pbase)
            fm_groups(wt, r_w, KC, 4, hfn, hres, sink)
            if n0 == OFF_SCX + 512:
                k.dma(sp, ctl[par].ap().rearrange("r (q c) -> (r q) c", c=64),
                      TLS[:, :, :].rearrange("p a b -> p (a b)"), rd=[r_tls], wr=[r_ctl[par]])
                pending.append((ctl[par], ctg[par], r_ctl[par], r_ctg[par]))
        while pending:
            a_, b_, ra_, rb_ = pending.pop(0)
            k.allgather(a_, b_, rd=[ra_], wr=[rb_])
        if STOP == "proj":
            return
        attention(l, par, QT)
        if STOP == "attn":
            return
        lru_sc(l, par)
        if STOP == "lru":
            return
        merge(l)
        if STOP == "merge":
            return
        mlp(l)

    def attention(l, par, QT):
        KG = RX[:, 0:3072].rearrange("p (r t) -> p r t", r=3)
        VW = 130
        VG = RX[:, 3072:3072 + 24 * VW].rearrange("p (a d) -> p a d", a=24)
        KL = RX[:, 6192:7216]
        KH = RX[:, 7216:7344]
        VL = RX[:, 7344:7344 + 8 * VW].rearrange("p (a d) -> p a d", a=8)
        VH = RX[:, 8384:8384 + VW]
        KSB = RX[:, 8520:8536]
        PB = [RX[:, 9216 + i * 512: 9216 + (i + 1) * 512] for i in range(4)]
        PT = [RX[:, 11264 + i * 512: 11264 + (i + 1) * 512].rearrange("p (a q) -> p a q", a=4) for i in range(4)]
        AO = RX[:, 13312:13440]
        r_kg = [Res() for _ in range(3)]
        r_vg = [Res() for _ in range(3)]
        r_kl, r_vl, r_kh, r_vh, r_ksb = Res(), Res(), Res(), Res(), Res()
        r_pb = [Res() for _ in range(4)]
        r_pt = [Res() for _ in range(4)]
        r_ao = Res()
        enter("RX", r_kg + r_vg + [r_kl, r_vl, r_kh, r_vh, r_ksb, r_ao] + r_pb + r_pt)
        r_sm2 = [Res(), Res()]
        r_rs2 = [Res(), Res()]
        r_ksf = Res()
        TMP = [WK[:, i_ * 256:(i_ + 1) * 256].bitcast(F32) for i_ in range(4)]
        BDG = WK[:, 1024:2048].rearrange("p (h q) -> p h q", h=NH)
        BPV = WK[:, 2048:3072].rearrange("p (h q) -> p h q", h=NH)
        r_tmp = [Res() for _ in range(4)]
        r_bm = Res()
        enter("WK", r_tmp + [r_bm])
        k.dma(sp, BDG, bdiag_in.rearrange("p (h q) -> p h q", h=NH), wr=[r_bm])
        k.dma(sp, BPV, bprev_in.rearrange("p (h q) -> p h q", h=NH), wr=[r_bm])
        tmp_i = [0]
        pg_i = [0]
        k.op(dve, lambda e: e.memset(VG[:, :, 128:VW], 1.0), wr=r_vg)
        k.op(dve, lambda e: e.memset(VL[:, :, 128:VW], 1.0), wr=[r_vl])
        AH = int(_os.environ.get("K_AH", NH))
        AT = int(_os.environ.get("K_AT", 8))
        AST = int(_os.environ.get("K_ASTAGE", 9))
        for h in range(AH):
            hg, hl = h // 4, h % 4
            for r in range(3):
                k.dma(sp, KG[:, r, :], ckg[par][hg].ap()[r * 512 + hl * 128: r * 512 + (hl + 1) * 128, :],
                      rd=[r_ckg[par][hg]], wr=[r_kg[r]])
                k.dma(sp, VG[:, r * 8:(r + 1) * 8, 0:128],
                      cvg[par][hg].ap()[r * 1024:(r + 1) * 1024, hl * 128:(hl + 1) * 128]
                      .rearrange("(a p) d -> p a d", p=128),
                      rd=[r_cvg[par][hg]], wr=[r_vg[r]])
            k.dma(sp, KL, ck[par][hg].ap()[hl * 128:(hl + 1) * 128, :], rd=[r_ck[par][hg]], wr=[r_kl])
            k.dma(sp, VL[:, :, 0:128], cv[par][hg].ap()[:, hl * 128:(hl + 1) * 128].rearrange("(a p) d -> p a d", p=128),
                  rd=[r_cv[par][hg]], wr=[r_vl])
            k.op(dve, lambda e: e.tensor_scalar(out=KH, in0=KG[:, 0, 896:1024], scalar1=col(C_OHP), scalar2=None,
                                                op0=ALU.mult), rd=[r_kg[0]], wr=[r_kh])
            k.op(dve, lambda e: e.tensor_scalar(out=VH, in0=VG[:, 7, :], scalar1=col(C_OHP), scalar2=None,
                                                op0=ALU.mult), rd=[r_vg[0]], wr=[r_vh])
            for r in (1, 2):
                k.op(dve, lambda e: e.scalar_tensor_tensor(out=KH, in0=KG[:, r, 896:1024], scalar=col(C_OHP + r),
                                                           in1=KH, op0=ALU.mult, op1=ALU.add),
                     rd=[r_kg[r]], wr=[r_kh])
                k.op(dve, lambda e: e.scalar_tensor_tensor(out=VH, in0=VG[:, r * 8 + 7, :], scalar=col(C_OHP + r),
                                                           in1=VH, op0=ALU.mult, op1=ALU.add),
                     rd=[r_vg[r]], wr=[r_vh])
            KSF = SM[:, 456:472]
            k.op(dve, lambda e: e.tensor_reduce(out=KSF[:, 0:4], in_=KL.rearrange("p (b s) -> p b s", b=4),
                                                axis=AX.X, op=ALU.add), rd=[r_kl], wr=[r_ksf])
            k.op(dve, lambda e: e.tensor_reduce(out=KSF[:, 4:16], in_=RX[:, 0:3072].rearrange("p (b s) -> p b s", b=12),
                                                axis=AX.X, op=ALU.add), rd=r_kg, wr=[r_ksf])
            k.op(dve, lambda e: e.tensor_copy(out=KSB, in_=KSF), rd=[r_ksf], wr=[r_ksb])
            def sm_views(t):
                so = (t % 2) * 168
                return dict(GM=SM[:, so + 0:so + 16], TOP=SM[:, so + 16:so + 24], THR=SM[:, so + 24:so + 25],
                            SEL=SM[:, so + 32:so + 48], BFAR=SM[:, so + 48:so + 64], SBN=SM[:, so + 64:so + 80],
                            BH=SM[:, so + 80:so + 81], BK=SM[:, so + 81:so + 84], RSC=SM[:, so + 96:so + 160],
                            RSUM=SM[:, so + 160:so + 161], RINV=SM[:, so + 161:so + 162])

            def G(t):
                v_ = sm_views(t)
                GM, TOP, THR, SEL, BFAR, SBN, BH, BK = (v_[x] for x in ("GM", "TOP", "THR", "SEL", "BFAR", "SBN", "BH", "BK"))
                r_sm = r_sm2[t % 2]
                qs = QT[:, h, t * 128:(t + 1) * 128]
                k.mm([lambda e: e.matmul(PS[:, 7, 0:16], lhsT=qs, rhs=KSB, start=True, stop=True)],
                     rd=[r_abc[2][h], r_ksb], wr=[r_ps[7]])
                k.op(dve, lambda e: e.tensor_tensor(out=GM, in0=PS[:, 7, 0:16], in1=MROW[:, t * 16:(t + 1) * 16],
                                                    op=ALU.add), rd=[r_ps[7]], wr=[r_sm])
                k.op(dve, lambda e: e.max(out=TOP, in_=GM), wr=[r_sm])
                k.op(dve, lambda e: e.tensor_scalar_max(out=THR, in0=TOP[:, 2:3], scalar1=-1.0e29), wr=[r_sm])
                k.op(dve, lambda e: e.tensor_scalar(out=SEL, in0=GM, scalar1=THR, scalar2=1.0, op0=ALU.is_ge,
                                                    op1=ALU.subtract), wr=[r_sm])
                k.op(dve, lambda e: e.tensor_scalar(out=BFAR, in0=SEL, scalar1=1.0e30, scalar2=col(C_CFAR + h),
                                                    op0=ALU.mult, op1=ALU.add), wr=[r_sm])
                k.op(dve, lambda e: e.tensor_scalar(out=SBN, in0=SEL, scalar1=1.0e30, scalar2=None, op0=ALU.mult),
                     wr=[r_sm])
                if t == 0:
                    k.op(dve, lambda e: e.tensor_scalar(out=BH, in0=SBN[:, 7:8], scalar1=col(C_OHP), scalar2=col(C_HNONE),
                                                        op0=ALU.mult, op1=ALU.add), wr=[r_sm])
                    for r in (1, 2):
                        k.op(dve, lambda e: e.scalar_tensor_tensor(out=BH, in0=SBN[:, 7 + 4 * r:8 + 4 * r],
                                                                   scalar=col(C_OHP + r), in1=BH, op0=ALU.mult,
                                                                   op1=ALU.add), wr=[r_sm])
                    for r in range(3):
                        k.op(dve, lambda e: e.tensor_scalar(out=BK[:, r:r + 1], in0=BFAR[:, 7 + 4 * r:8 + 4 * r],
                                                            scalar1=col(C_HKILL + r), scalar2=None, op0=ALU.add),
                             wr=[r_sm])

            def tile_groups(t):
                v_ = sm_views(t)
                BFAR, SBN, BH, BK = v_["BFAR"], v_["SBN"], v_["BH"], v_["BK"]
                own_tiles = []
                for kt in range(t + 1):
                    if kt == t:
                        own_tiles.append((VL[:, kt, :], "diag", None, None))
                    elif kt == t - 1:
                        bl = kt // 2
                        if t % 2 == 1:
                            own_tiles.append((VL[:, kt, :], "prev", None, None))
                        else:
                            own_tiles.append((VL[:, kt, :], "prev", ("s", bl), SBN[:, bl:bl + 1]))
                    else:
                        own_tiles.append((VL[:, kt, :], "far", ("f", kt // 2), BFAR[:, kt // 2: kt // 2 + 1]))
                groups = []
                for g0 in range(0, t + 1, 4):
                    g1 = min(g0 + 4, t + 1)
                    groups.append((KL[:, g0 * 128:(g0 + 4) * 128], own_tiles[g0:g1], r_kl, r_vl))
                if t == 0:
                    groups.append((KH, [(VH, "prev", ("h", 0), BH)], r_kh, r_vh))
                for r in range(3):
                    for half in range(2):
                        tl = []
                        for i in range(4):
                            kt = half * 4 + i
                            n = 4 + 4 * r + kt // 2
                            if t == 0 and kt == 7:
                                tl.append((VG[:, r * 8 + kt, :], "far", ("k", r), BK[:, r:r + 1]))
                            else:
                                tl.append((VG[:, r * 8 + kt, :], "far", ("f", n), BFAR[:, n:n + 1]))
                        groups.append((KG[:, r, half * 512:(half + 1) * 512], tl, r_kg[r], r_vg[r]))
                return groups

            items = []
            tstate = {}
            for t in range(AT):
                gs = tile_groups(t)
                tstate[t] = dict(nex=0, ntiles=sum(len(g[1]) for g in gs), tile_no=0, last_exp=None, ngroups=len(gs))
                for gi, g in enumerate(gs):
                    items.append(dict(t=t, gi=gi, g=g))

            def A(it, idx):
                t = it["t"]
                ks, tl, rk_, rv_ = it["g"]
                v_ = sm_views(t)
                RSC = v_["RSC"]
                r_sm, r_rs = r_sm2[t % 2], r_rs2[t % 2]
                st = tstate[t]
                qs = QT[:, h, t * 128:(t + 1) * 128]
                n = len(tl)
                sbk = (0, 1, 2)[idx % 3]
                pi_ = idx % 4
                it["sbk"], it["pi"] = sbk, pi_
                nk_ = ks.shape[-1]
                k.mm([lambda e: e.matmul(PS[:, sbk, 0:nk_], lhsT=qs, rhs=ks, start=True, stop=True)],
                     rd=[r_abc[2][h], rk_], wr=[r_ps[sbk]])
                P = PB[pi_]
                tmp_of = {}
                for i in range(n):
                    vsrc, kind, bkey, bap = tl[i]
                    if kind != "far":
                        ti = tmp_i[0] % 4
                        tmp_i[0] += 1
                        tmp_of[i] = ti
                        bm = BPV if kind == "prev" else BDG
                        k.op(dve, lambda e: e.scalar_tensor_tensor(
                            out=TMP[ti], in0=PS[:, sbk, i * 128:(i + 1) * 128], scalar=SCALE, in1=bm[:, h, :],
                            op0=ALU.mult, op1=ALU.add), rd=[r_ps[sbk], r_bm], wr=[r_tmp[ti]])
                extra_rd = [r_tmp[ti_] for ti_ in set(tmp_of.values())]
                i = 0
                firstg = True
                last_exp = None
                while i < n:
                    vsrc, kind, bkey, bap = tl[i]
                    j = i + 1
                    wr_ = ([r_pb[pi_]] if firstg else [])
                    if kind == "far":
                        while j < n and tl[j][1] == "far" and tl[j][2] == bkey:
                            j += 1
                        last_exp = k.op(act, lambda e: e.activation(
                            out=P[:, i * 128:j * 128], in_=PS[:, sbk, i * 128:j * 128], func=AF.Exp,
                            bias=bap, scale=SCALE),
                            rd=[r_ps[sbk], r_sm] + extra_rd, wr=wr_)
                    else:
                        ti = tmp_of[i]
                        last_exp = k.op(act, lambda e: e.activation(
                            out=P[:, i * 128:(i + 1) * 128], in_=TMP[ti], func=AF.Exp,
                            bias=(bap if bap is not None else 0.0), scale=1.0),
                            rd=[r_tmp[ti], r_sm], wr=wr_)
                    firstg = False
                    st["nex"] += 1
                    i = j
                r_pb[pi_].w = last_exp
                r_pb[pi_].rd = []
                r_ps[sbk].rd.append(last_exp)
                st["last_exp"] = last_exp

            def B(it, idx):
                ks, tl, rk_, rv_ = it["g"]
                n = len(tl)
                pi_ = it["pi"]
                P = PB[pi_]
                tb = (3, 4)[idx % 2]
                PSB = PS[:, tb, :].bitcast(BF16)
                k.mm([lambda e, i=i: e.transpose(out=PSB[:, i * 128:(i + 1) * 128], in_=P[:, i * 128:(i + 1) * 128],
                                                 identity=IDN[:, :]) for i in range(n)],
                     rd=[r_pb[pi_]], wr=[r_ps[tb]])
                k.op(dve, lambda e: e.tensor_copy(out=PT[pi_][:, 0:n, :].rearrange("p a q -> p (a q)"),
                                                  in_=PSB[:, 0:n * 128]), rd=[r_ps[tb]], wr=[r_pt[pi_]])

            def C(it, idx):
                t = it["t"]
                ks, tl, rk_, rv_ = it["g"]
                n = len(tl)
                pi_ = it["pi"]
                st = tstate[t]
                ob = (5, 6)[t % 2]
                tn0 = st["tile_no"]
                k.mm([lambda e, i=i: e.matmul(PS[:, ob, 0:129], lhsT=PT[pi_][:, i, :], rhs=tl[i][0][:, 0:129],
                                              start=(tn0 + i == 0), stop=(tn0 + i == st["ntiles"] - 1))
                      for i in range(n)], rd=[r_pt[pi_], rv_], wr=[r_ps[ob]])
                st["tile_no"] += n
                if it["gi"] == st["ngroups"] - 1:
                    F(t)

            def F(t):
                v_ = sm_views(t)
                RSC, RSUM, RINV = v_["RSC"], v_["RSUM"], v_["RINV"]
                r_sm, r_rs = r_sm2[t % 2], r_rs2[t % 2]
                st = tstate[t]
                ob = (5, 6)[t % 2]
                k.op(dve, lambda e: e.reciprocal(RINV, PS[:, ob, 128:129]), rd=[r_ps[ob]], wr=[r_sm])
                k.op(act, lambda e: e.activation(out=AO, in_=PS[:, ob, 0:128], func=AF.Copy, scale=RINV),
                     rd=[r_ps[ob], r_sm], wr=[r_ao])
                PSB7 = PS[:, 7, :].bitcast(BF16)
                k.mm([lambda e: e.transpose(out=PSB7[:, 512:640], in_=AO, identity=IDN[:, :])],
                     rd=[r_ao], wr=[r_ps[7]])
                k.op(act, lambda e: e.copy(out=ABC[:, 0, h, t * 128:(t + 1) * 128], in_=PSB7[:, 512:640]),
                     rd=[r_ps[7]], wr=[r_abc[0][h]])

            if AT > 0:
                G(0)
            NI = len(items)
            DB_, DC_ = 3, 6
            for idx in range(NI + DC_):
                if idx < NI:
                    it = items[idx]
                    if it["gi"] == 0 and it["t"] + 1 < AT:
                        G(it["t"] + 1)
                    A(it, idx)
                if 0 <= idx - DB_ < NI:
                    B(items[idx - DB_], idx - DB_)
                if 0 <= idx - DC_ < NI:
                    C(items[idx - DC_], idx - DC_)

    def lru_sc(l, par):
        cb = l * LC
        FW = [RX[:, 8192 + i * 2048: 8192 + (i + 1) * 2048].bitcast(F32) for i in range(4)]
        r_fw = [Res() for _ in range(4)]
        P2 = RX[:, 0:8192].rearrange("p (c t) -> p c t", c=8)
        r_p2 = [Res() for _ in range(8)]
        enter("RX", r_fw + r_p2)
        XR = WK[:, 0:1027]
        XCB = WK[:, 1032:2056]
        GRB = WK[:, 2056:3080]
        GB = WK[:, 3080:4104]
        CX = WK[:, 4104:5130]
        SB_ = WK[:, 5136:6160]
        SCX_ = WK[:, 6160:7184]
        TG = WK[:, 7184:7184 + 4 * 64].rearrange("p (r c) -> p r c", r=4)
        r_xr, r_xcb, r_grb, r_gb, r_cx, r_sb, r_scx, r_tg = [Res() for _ in range(8)]
        enter("WK", [r_xr, r_xcb, r_grb, r_gb, r_cx, r_sb, r_scx, r_tg])
        CCH = SM[:, 344:352]
        CAR = SM[:, 352:368]
        CG = SM[:, 368:432].rearrange("p (r c) -> p r c", r=4)
        HS = SM[:, 336:340]
        r_car = Res()
        r_cch = Res()
        r_hs = Res()
        r_cg = Res()
        for r in range(3):
            k.dma(sp, TG[:, r, :], ctg[par].ap()[r * 8:(r + 1) * 8, :].rearrange("b (q c) -> (b q) c", c=64),
                  rd=[r_ctg[par]], wr=[r_tg])
        TGv = TG.rearrange("p r (a b) -> p r a b", a=8)
        k.op(act, lambda e: e.activation(out=CCH, in_=COLS[:, cb + C_LAM: cb + C_LAM + 8], func=AF.Exp, scale=-1.0),
             rd=[r_const], wr=[r_cch])
        k.op(act, lambda e: e.activation(out=CCH, in_=CCH, func=AF.Ln, bias=1.0, scale=1.0), wr=[r_cch])
        k.op(dve, lambda e: e.tensor_scalar(out=CCH, in0=CCH, scalar1=-8.0, scalar2=None, op0=ALU.mult), wr=[r_cch])
        XC, RA, IU, MA = FW
        for ct in range(8):
            k.dma(sp, XR[:, 3:1027], spill[SP_XR, ct * 128:(ct + 1) * 128, :], rd=[r_spill[SP_XR][ct]], wr=[r_xr])
            k.dma(sp, GRB, spill[SP_GR, ct * 128:(ct + 1) * 128, :], rd=[r_spill[SP_GR][ct]], wr=[r_grb])
            k.op(dve, lambda e: e.tensor_scalar(out=XR[:, 0:3], in0=TGv[:, 0, ct, 0:3], scalar1=col(C_OHP), scalar2=None,
                                                op0=ALU.mult), rd=[r_tg, r_const], wr=[r_xr])
            for r in (1, 2):
                k.op(dve, lambda e: e.scalar_tensor_tensor(out=XR[:, 0:3], in0=TGv[:, r, ct, 0:3],
                                                           scalar=col(C_OHP + r), in1=XR[:, 0:3], op0=ALU.mult,
                                                           op1=ALU.add), rd=[r_tg, r_const], wr=[r_xr])
            k.op(dve, lambda e: e.tensor_scalar(out=XC, in0=XR[:, 0:1024], scalar1=col(cb + C_CW + ct * 4),
                                                scalar2=col(cb + C_CB + ct), op0=ALU.mult, op1=ALU.add),
                 rd=[r_xr, r_const], wr=[r_fw[0]])
            for kk in range(1, 4):
                k.op(dve, lambda e: e.scalar_tensor_tensor(out=XC, in0=XR[:, kk:kk + 1024],
                                                           scalar=col(cb + C_CW + ct * 4 + kk), in1=XC,
                                                           op0=ALU.mult, op1=ALU.add),
                     rd=[r_xr, r_const], wr=[r_fw[0]])
            k.op(act, lambda e: e.copy(out=XCB, in_=XC), rd=[r_fw[0]], wr=[r_xcb])
            bA = [bank() for _ in range(4)]
            for th in range(2):
                k.mm([lambda e: e.matmul(PS[:, bA[th], :], lhsT=LW[:, 0, ct, :], rhs=XCB[:, th * 512:(th + 1) * 512],
                                         start=True, stop=True)], rd=[r_lw, r_xcb], wr=[r_ps[bA[th]]])
                k.mm([lambda e: e.matmul(PS[:, bA[2 + th], :], lhsT=LW[:, 1, ct, :], rhs=XCB[:, th * 512:(th + 1) * 512],
                                         start=True, stop=True)], rd=[r_lw, r_xcb], wr=[r_ps[bA[2 + th]]])
            for th in range(2):
                k.op(act, lambda e: e.activation(out=RA[:, th * 512:(th + 1) * 512], in_=PS[:, bA[th], :],
                                                 func=AF.Sigmoid, bias=col(cb + C_BA + ct), scale=1.0),
                     rd=[r_ps[bA[th]], r_const], wr=[r_fw[1]])
                k.op(act, lambda e: e.activation(out=IU[:, th * 512:(th + 1) * 512], in_=PS[:, bA[2 + th], :],
                                                 func=AF.Sigmoid, bias=col(cb + C_BX + ct), scale=1.0),
                     rd=[r_ps[bA[2 + th]], r_const], wr=[r_fw[2]])
            k.op(act, lambda e: e.activation(out=RA, in_=RA, func=AF.Exp, scale=CCH[:, ct:ct + 1]),
                 rd=[r_cch], wr=[r_fw[1]])
            k.op(dve, lambda e: e.tensor_tensor(out=MA, in0=RA, in1=RA, op=ALU.mult), rd=[r_fw[1]], wr=[r_fw[3]])
            k.op(act, lambda e: e.activation(out=MA, in_=MA, func=AF.Sqrt, scale=-1.0, bias=1.0), wr=[r_fw[3]])
            k.op(dve, lambda e: e.tensor_tensor(out=IU, in0=IU, in1=MA, op=ALU.mult), rd=[r_fw[3]], wr=[r_fw[2]])
            k.op(dve, lambda e: e.tensor_tensor(out=IU, in0=IU, in1=XC, op=ALU.mult), rd=[r_fw[0]], wr=[r_fw[2]])
            k.op(dve, lambda e: e.tensor_tensor_scan(out=XC, data0=RA, data1=IU, initial=0.0, op0=ALU.mult, op1=ALU.add),
                 rd=[r_fw[1], r_fw[2]], wr=[r_fw[0]])
            k.op(dve, lambda e: e.tensor_tensor_scan(out=MA, data0=RA, data1=ZC[:, 0:1].to_broadcast([128, T]),
                                                     initial=1.0, op0=ALU.mult, op1=ALU.add),
                 rd=[r_fw[1], r_const], wr=[r_fw[3]])
            k.op(act, lambda e: e.activation(out=GB, in_=GRB, func=AF.Gelu_apprx_tanh), rd=[r_grb], wr=[r_gb])
            k.op(dve, lambda e: e.tensor_tensor(out=ABC[:, 1, ct, :], in0=XC, in1=GB, op=ALU.mult),
                 rd=[r_fw[0], r_gb], wr=[r_abc[1][ct]])
            k.op(dve, lambda e: e.tensor_tensor(out=P2[:, ct, :], in0=MA, in1=GB, op=ALU.mult),
                 rd=[r_fw[3], r_gb], wr=[r_p2[ct]])
            k.op(act, lambda e: e.copy(out=CAR[:, ct:ct + 1], in_=MA[:, 1023:1024]), rd=[r_fw[3]], wr=[r_car])
            k.op(act, lambda e: e.copy(out=CAR[:, 8 + ct:9 + ct], in_=XC[:, 1023:1024]), rd=[r_fw[0]], wr=[r_car])
            k.dma(sp, SB_, spill[SP_SCB, ct * 128:(ct + 1) * 128, :], rd=[r_spill[SP_SCB][ct]], wr=[r_sb])
            k.dma(sp, CX[:, 2:1026], spill[SP_SCC, ct * 128:(ct + 1) * 128, :], rd=[r_spill[SP_SCC][ct]], wr=[r_cx])
            k.dma(sp, SCX_, spill[SP_SCX, ct * 128:(ct + 1) * 128, :], rd=[r_spill[SP_SCX][ct]], wr=[r_scx])
            k.op(dve, lambda e: e.tensor_tensor(out=CX[:, 2:1026], in0=CX[:, 2:1026], in1=SCX_, op=ALU.mult),
                 rd=[r_scx], wr=[r_cx])
            for r in range(3):
                k.op(dve, lambda e: e.tensor_tensor(out=HS[:, 0:2], in0=TGv[:, r, ct, 3:5], in1=TGv[:, r, ct, 5:7],
                                                    op=ALU.mult), rd=[r_tg], wr=[r_hs])
                if r == 0:
                    k.op(dve, lambda e: e.tensor_scalar(out=CX[:, 0:2], in0=HS[:, 0:2], scalar1=col(C_OHP), scalar2=None,
                                                        op0=ALU.mult), rd=[r_hs, r_const], wr=[r_cx])
                else:
                    k.op(dve, lambda e: e.scalar_tensor_tensor(out=CX[:, 0:2], in0=HS[:, 0:2], scalar=col(C_OHP + r),
                                                               in1=CX[:, 0:2], op0=ALU.mult, op1=ALU.add),
                         rd=[r_hs, r_const], wr=[r_cx])
            Y = IU
            k.op(dve, lambda e: e.tensor_scalar(out=Y, in0=CX[:, 0:1024], scalar1=col(cb + C_SW + ct * 3), scalar2=None,
                                                op0=ALU.mult), rd=[r_cx, r_const], wr=[r_fw[2]])
            for kk in (1, 2):
                k.op(dve, lambda e: e.scalar_tensor_tensor(out=Y, in0=CX[:, kk:kk + 1024],
                                                           scalar=col(cb + C_SW + ct * 3 + kk), in1=Y,
                                                           op0=ALU.mult, op1=ALU.add),
                     rd=[r_cx, r_const], wr=[r_fw[2]])
            k.op(dve, lambda e: e.tensor_tensor(out=ABC[:, 2, ct, :], in0=Y, in1=SB_, op=ALU.mult),
                 rd=[r_fw[2], r_sb], wr=[r_abc[2][ct]])
        k.dma(sp, cin2[par].ap()[:, :], CAR, rd=[r_car], wr=[r_cin2[par]])
        k.allgather(cin2[par], cout2[par], rd=[r_cin2[par]], wr=[r_cout2[par]])
        if l + 1 < depth:
            gather_weights(l + 1)
        k.dma(sp, CG, cout2[par].ap().rearrange("(r p) c -> p r c", p=128), rd=[r_cout2[par]], wr=[r_cg])
        ST = SM[:, 432:440]
        AP_ = SM[:, 440:448]
        HP_ = SM[:, 448:456]
        r_st = Res()
        k.op(dve, lambda e: e.memset(ST, 0.0), wr=[r_st])
        for r in range(3):
            k.op(dve, lambda e: e.tensor_scalar(out=AP_, in0=CG[:, r, 0:8], scalar1=col(C_MLT + r),
                                                scalar2=col(C_NML + r), op0=ALU.mult, op1=ALU.add),
                 rd=[r_cg, r_const], wr=[r_st])
            k.op(dve, lambda e: e.tensor_scalar(out=HP_, in0=CG[:, r, 8:16], scalar1=col(C_MLT + r), scalar2=None,
                                                op0=ALU.mult), rd=[r_cg, r_const], wr=[r_st])
            k.op(dve, lambda e: e.tensor_tensor(out=ST, in0=ST, in1=AP_, op=ALU.mult), wr=[r_st])
            k.op(dve, lambda e: e.tensor_tensor(out=ST, in0=ST, in1=HP_, op=ALU.add), wr=[r_st])
        for ct in range(8):
            k.op(dve, lambda e: e.scalar_tensor_tensor(out=ABC[:, 1, ct, :], in0=P2[:, ct, :], scalar=ST[:, ct:ct + 1],
                                                       in1=ABC[:, 1, ct, :], op0=ALU.mult, op1=ALU.add),
                 rd=[r_p2[ct], r_st], wr=[r_abc[1][ct]])

    def add_into_xt(n, th, b):
        xv_ = XT[:, n, th * 512:(th + 1) * 512]
        k.op(dve, lambda e: e.tensor_tensor(out=xv_, in0=PS[:, b, :], in1=xv_, op=ALU.add),
             rd=[r_ps[b]], wr=[r_xt[n][th]])

    def merge(l):
        cb = l * LC
        enter("RX", r_ht)
        k.dma(sp, HT[:, :, :], hsp.rearrange("(c p) t -> p c t", p=128), rd=[r_hsp], wr=list(r_ht))
        SG = WK[:, 0:4096].rearrange("p (c t) -> p c t", c=4)
        MT = WK[:, 4096:8192].rearrange("p (c t) -> p c t", c=4)
        TF = [WK[:, 8192 + i * 1024: 8192 + (i + 1) * 1024].bitcast(F32) for i in range(2)]
        r_sg = [[Res(), Res()] for _ in range(4)]
        r_mt = [[Res(), Res()] for _ in range(4)]
        r_tf = [Res(), Res()]
        enter("WK", [x for y in r_sg + r_mt for x in y] + r_tf)
        hres = list(r_ht)
        hfn = lambda kc: HT[:, kc, :]
        tfi = [0]
        for sc in range(4):
            for br in range(3):
                wt, r_w = ws.get()

                def gsink(ncn, th, b):
                    k.op(act, lambda e: e.activation(out=SG[:, ncn, th * 512:(th + 1) * 512], in_=PS[:, b, :],
                                                     func=AF.Sigmoid,
                                                     bias=col(cb + C_GB + br * 16 + sc * 4 + ncn), scale=1.0),
                         rd=[r_ps[b], r_const], wr=[r_sg[ncn][th]])
                fm_groups(wt, r_w, KC, 4, hfn, hres, gsink)
                wt, r_w = ws.get()

                def ysink(ncn, th, b):
                    mv = MT[:, ncn, th * 512:(th + 1) * 512]
                    sv = SG[:, ncn, th * 512:(th + 1) * 512]
                    if br == 0:
                        k.op(dve, lambda e: e.tensor_tensor(out=mv, in0=PS[:, b, :], in1=sv, op=ALU.mult),
                             rd=[r_ps[b], r_sg[ncn][th]], wr=[r_mt[ncn][th]])
                    else:
                        ti = tfi[0] % 2
                        tfi[0] += 1
                        k.op(dve, lambda e: e.tensor_tensor(out=TF[ti], in0=PS[:, b, :], in1=sv, op=ALU.mult),
                             rd=[r_ps[b], r_sg[ncn][th]], wr=[r_tf[ti]])
                        k.op(dve, lambda e: e.tensor_tensor(out=mv, in0=TF[ti], in1=mv, op=ALU.add),
                             rd=[r_tf[ti]], wr=[r_mt[ncn][th]])
                fm_groups(wt, r_w, 8, 4, lambda kc: ABC[:, br, kc, :], list(r_abc[br]), ysink)
            wt, r_w = ws.get()
            for n in range(16):
                bb = [bank(), bank()]
                k.mm([lambda e, kc=kc, th=th: e.matmul(PS[:, bb[th], :], lhsT=wt[:, kc, n * 128:(n + 1) * 128],
                                                       rhs=MT[:, kc, th * 512:(th + 1) * 512],
                                                       start=(kc == 0), stop=(kc == 3))
                      for kc in range(4) for th in range(2)],
                     rd=r_w + [r_mt[c][th] for c in range(4) for th in range(2)],
                     wr=[r_ps[bb[0]], r_ps[bb[1]]])
                for th in range(2):
                    add_into_xt(n, th, bb[th])

    def mlp(l):
        cb = l * LC
        norm_to_ht(cb + C_G2)
        RT = [WK[:, 6144 + i * 1024: 6144 + (i + 1) * 1024].bitcast(F32) for i in range(2)]
        r_rt = [Res(), Res()]
        enter("WK", r_rt)
        hres = list(r_ht)
        hfn = lambda kc: HT[:, kc, :]
        rti = [0]
        for sp_ in range(8):
            us = sp_ % 2
            for hh in range(2):
                wt, r_w = ws.get()

                def usink(ncn, th, b):
                    ti = rti[0] % 2
                    rti[0] += 1
                    uc = hh * 4 + ncn
                    k.op(act, lambda e: e.activation(out=RT[ti], in_=PS[:, b, :], func=AF.Relu),
                         rd=[r_ps[b]], wr=[r_rt[ti]])
                    k.op(dve, lambda e: e.tensor_tensor(out=ABC[:, us, uc, th * 512:(th + 1) * 512], in0=RT[ti],
                                                        in1=RT[ti], op=ALU.mult), rd=[r_rt[ti]], wr=[r_abc[us][uc]])
                fm_groups(wt, r_w, KC, 4, hfn, hres, usink)
            for hh in range(2):
                wt, r_w = ws.get()
                for n8 in range(8):
                    bb = [bank(), bank()]
                    k.mm([lambda e, kc=kc, th=th: e.matmul(PS[:, bb[th], :], lhsT=wt[:, kc, n8 * 128:(n8 + 1) * 128],
                                                           rhs=ABC[:, us, kc, th * 512:(th + 1) * 512],
                                                           start=(kc == 0), stop=(kc == 7))
                          for kc in range(8) for th in range(2)],
                         rd=r_w + list(r_abc[us]), wr=[r_ps[bb[0]], r_ps[bb[1]]])
                    for th in range(2):
                        add_into_xt(hh * 8 + n8, th, bb[th])

    import os
    STOP = os.environ.get("K_STOP", "")
    if DBG == "attn":
        QT = ABC[:, 2]
        k.dma(sp, QT, qT_dbg.rearrange("(h p) t -> p h t", p=128), wr=list(r_abc[2]))
        attention(0, 0, QT)
        dt_ = k.dma(sp, aT_dbg.rearrange("(h p) t -> p h t", p=128), ABC[:, 0], rd=list(r_abc[0]))
        k._wait(sp, [dt_])
        STOP = "none"
    gather_weights(0)
    for l in range(depth):
        if STOP == "none":
            break
        layer(l)
    OB = [WK[:, 6144 + i * 2048: 6144 + (i + 1) * 2048].bitcast(F32) for i in range(2)]
    r_ob = [Res(), Res()]
    out_toks = []

    def femit(c, RS, r_rs):
        s = c % 2
        k.op(dve, lambda e: e.scalar_tensor_tensor(out=OB[s], in0=XT[:, c, :], scalar=col(C_FG + c), in1=RS,
                                                   op0=ALU.mult, op1=ALU.mult),
             rd=r_xt[c] + [r_rs, r_const], wr=[r_ob[s]])
        out_toks.append(k.dma(sp, outT[c * 128:(c + 1) * 128, :], OB[s], rd=[r_ob[s]]))
    if os.environ.get("K_NOFINAL"):
        for c in range(KC):
            out_toks.append(k.dma(sp, outT[c * 128:(c + 1) * 128, :], XT[:, c, :], rd=r_xt[c]))
    else:
        rmsnorm(femit, extra=r_ob)
    k._wait(sp, out_toks)
    if k.cccnt:
        k._wait(sp, [(k.ccsem, k.cccnt)])
        k._wait(pool, [(k.ccsem, k.cccnt)])
    assert STOP or ws.used == len(ws.plan), (ws.used, len(ws.plan))
    return nc, es


def _t5_bucket(n):
    n = np.maximum(n, 0)
    max_exact = 16
    nf = np.maximum(n, 1).astype(np.float32)
    large = max_exact + (np.log(nf / np.float32(max_exact)) / np.float32(math.log(128 / max_exact))
                         * np.float32(32 - max_exact)).astype(np.int32)
    large = np.minimum(large, 31)
    return np.where(n < max_exact, n, large)


def _host_consts(inp, core):
    j = core % 4
    f32 = np.float32
    cols = np.zeros((128, NCOL), f32)

    def put(c0, vec):
        v = np.asarray(vec, f32).reshape(-1, 128).T
        cols[:, c0:c0 + v.shape[1]] = v

    for l in range(DEPTH):
        cb = l * LC
        put(cb + C_G1, inp["norm_mix_g"][l])
        put(cb + C_G2, inp["norm_mlp_g"][l])
        put(cb + C_GB, inp["gate_b"][l])
        cw = np.asarray(inp["lru_conv_w"][l], f32)
        cols[:, cb + C_CW: cb + C_CW + 32] = cw.reshape(4, 8, 128).transpose(2, 1, 0).reshape(128, 32)
        put(cb + C_CB, inp["lru_conv_b"][l])
        put(cb + C_BA, inp["lru_ba"][l])
        put(cb + C_BX, inp["lru_bx"][l])
        put(cb + C_LAM, inp["lru_lambda"][l])
        sw = np.asarray(inp["sc_conv_w"][l], f32)
        cols[:, cb + C_SW: cb + C_SW + 24] = sw.reshape(3, 8, 128).transpose(2, 1, 0).reshape(128, 24)
    put(C_FG, inp["final_g"])
    rel = np.asarray(inp["rel_table"], f32)
    cols[:, C_CFAR:C_CFAR + 8] = rel[31][None, :]
    for r in range(3):
        cols[:, C_OHP + r] = 1.0 if r == j - 1 else 0.0
        cols[:, C_HKILL + r] = NEG if r == j - 1 else 0.0
    for r in range(4):
        cols[:, C_MLT + r] = 1.0 if r < j else 0.0
        cols[:, C_NML + r] = 0.0 if r < j else 1.0
    cols[:, C_HNONE] = NEG if j == 0 else 0.0
    mrow = np.zeros((128, 128), f32)
    for t in range(8):
        for b in range(4):
            mrow[:, t * 16 + b] = 0.0 if b < t // 2 else NEG
        for n in range(12):
            mrow[:, t * 16 + 4 + n] = 0.0 if n < 4 * j else NEG
    ext = np.concatenate([rel, np.full((1, NH), NEG, f32)], axis=0)
    q = np.arange(128)[:, None]
    kk = np.arange(128)[None, :]
    idx_d = np.where(q >= kk, _t5_bucket(q - kk), 32)
    idx_p = _t5_bucket(128 + q - kk)
    bdiag = ext[idx_d].transpose(0, 2, 1).reshape(128, NH * 128).astype(ml_dtypes.bfloat16)
    bprev = ext[idx_p].transpose(0, 2, 1).reshape(128, NH * 128).astype(ml_dtypes.bfloat16)
    return cols, mrow, bdiag, bprev


_CACHE = {}


def kernel(**inp):
    depth = int(inp.pop("_depth", DEPTH))
    if depth not in _CACHE:
        _CACHE[depth] = build(depth)
    nc, es = _CACHE[depth]
    x = np.asarray(inp["x"], np.float32)
    shared = {}
    big = ("w_in", "w_attn_o", "w_lru_o", "w_sc_o", "w_out", "w_mlp_up", "w_mlp_down")
    for name in big + ("lru_wa", "lru_wx"):
        shared[name] = np.asarray(inp[name], np.float32)[:depth]
    ident = np.eye(128, dtype=np.float32).astype(ml_dtypes.bfloat16)
    in_maps = []
    for c in range(8):
        b, j = c // 4, c % 4
        cols, mrow, bdiag, bprev = _host_consts(inp, c)
        m = {}
        for name, w_ in shared.items():
            if GATHER and name in big:
                rr = w_.shape[1] // 8
                m[name] = np.ascontiguousarray(w_[:, c * rr:(c + 1) * rr, :])
            else:
                m[name] = np.ascontiguousarray(w_)
        m["xT"] = np.ascontiguousarray(x[b, j * T:(j + 1) * T, :].T)
        m["cols"] = cols
        m["mrow"] = mrow
        m["bdiag"] = bdiag
        m["bprev"] = bprev
        m["ident"] = ident
        in_maps.append(m)
    res = run_bass_kernel_spmd(nc, in_maps, core_ids=list(range(8)))
    out = np.empty((2, 4 * T, D), np.float32)
    for c in range(8):
        b, j = c // 4, c % 4
        out[b, j * T:(j + 1) * T, :] = res.results[c]["outT"].T
    return out
```
